# Optimizing a Trainium2 kernel written in Bass

```python
import math
import jax
import jax.numpy as jnp
from jax import lax
import numpy as np

D_MODEL = 1024
BATCH = 4
SEQ = 8192
DEPTH = 2

GRID_W = 64
CTX_LEN = 256
N_MOD = 6
BRANCH_WIDTH = 512
N_BRANCH = 3
DA_HEADS = 4
DA_QK_DIM = 64
DA_V_DIM = 2 * DA_QK_DIM
DA_QK_WIDTH = DA_HEADS * 2 * DA_QK_DIM
ROPE_BASE = 10000.0
Q_BLOCK = 128
LRU_BLOCKS = 8
LRU_BLOCK_DIM = BRANCH_WIDTH // LRU_BLOCKS
LRU_C = 8.0
CONV_K = 4
GDN_HEADS = 4
GDN_DK = 128
GDN_DV = 128
GDN_QKV_WIDTH = GDN_HEADS * (2 * GDN_DK + GDN_DV)
GDN_CHUNK = 64
N_DIR = 2
D_FF = 4 * D_MODEL
EPS = 1e-6
IN_SPLITS = (DA_QK_WIDTH, DA_QK_WIDTH, BRANCH_WIDTH,
             BRANCH_WIDTH, BRANCH_WIDTH,
             GDN_QKV_WIDTH, BRANCH_WIDTH,
             N_DIR * GDN_HEADS, N_DIR * GDN_HEADS,
             N_BRANCH * D_MODEL)
N_IN = sum(IN_SPLITS)

kernel_name = 'hybrid_prefix_dit_block'


def rms_norm(x, g):
    xf = x.astype(jnp.float32)
    y = xf * lax.rsqrt(jnp.mean(xf * xf, axis=-1, keepdims=True) + EPS)
    return (y * g.astype(jnp.float32)).astype(x.dtype)


def l2_norm(x):
    xf = x.astype(jnp.float32)
    return (xf * lax.rsqrt(jnp.sum(xf * xf, axis=-1, keepdims=True) + EPS)).astype(x.dtype)


def modulate(h, shift, scale):
    return h * (1 + scale) + shift


def flip_t(t):
    return t[:, ::-1]


def dwconv_centred(x, w, b=None):
    K = w.shape[0]
    T = x.shape[1]
    lo = (K - 1) // 2
    xp = jnp.pad(x, ((0, 0), (lo, K - 1 - lo), (0, 0)))
    y = sum(xp[:, k:k + T] * w[k] for k in range(K))
    return y if b is None else y + b


def split_proj(p):
    offs = np.cumsum(IN_SPLITS)[:-1].tolist()
    return jnp.split(p, offs, axis=-1)


def axial_rope_tables(row, col):
    n_freq = DA_QK_DIM // 4
    inv = ROPE_BASE ** (-jnp.arange(n_freq, dtype=jnp.float32) / n_freq)
    ang_r = row.astype(jnp.float32)[:, None] * inv
    ang_c = col.astype(jnp.float32)[:, None] * inv
    ang = jnp.concatenate([ang_r, ang_r, ang_c, ang_c], axis=-1)
    return jnp.cos(ang), jnp.sin(ang)


def apply_axial_rope(x, cos, sin):
    xf = x.astype(jnp.float32)
    xs = xf.reshape(x.shape[:-1] + (2, 2, DA_QK_DIM // 4))
    rot = jnp.stack([-xs[..., 1, :], xs[..., 0, :]], axis=-2).reshape(x.shape)
    c = cos[None, :, None, None, :]
    s = sin[None, :, None, None, :]
    return (xf * c + rot * s).astype(x.dtype)


def diff_attend(q, k, v, lam):
    s = jnp.einsum('bqhcd,bkhcd->bhcqk', q, k, preferred_element_type=jnp.float32) * (DA_QK_DIM ** -0.5)
    p = jax.nn.softmax(s, axis=-1)
    w = p[:, :, 0] - lam * p[:, :, 1]
    return jnp.einsum('bhqk,bkhv->bqhv', w.astype(v.dtype), v)


def diff_attend_blocks(q, k, v, lam):
    B, T = q.shape[:2]
    nb = T // Q_BLOCK
    qb = jnp.moveaxis(q.reshape((B, nb, Q_BLOCK) + q.shape[2:]), 1, 0)
    ob = lax.map(lambda qq: diff_attend(qq, k, v, lam), qb)
    return jnp.moveaxis(ob, 0, 1).reshape((B, T) + ob.shape[3:])


def da_branch(q_l, k_l, v_l, q_c, k_c, v_c, q_g, k_g, lam_vec, sub_g, lam_init, cos, sin, need_ctx):
    def heads(q, k, v):
        B, T, _ = q.shape
        q = rms_norm(q.reshape(B, T, DA_HEADS, 2, DA_QK_DIM), q_g)
        k = rms_norm(k.reshape(B, T, DA_HEADS, 2, DA_QK_DIM), k_g)
        return q, k, v.reshape(B, T, DA_HEADS, DA_V_DIM)

    def post(o):
        B, T = o.shape[:2]
        return (rms_norm(o, sub_g) * (1.0 - lam_init)).reshape(B, T, BRANCH_WIDTH)

    ql, kl, vl = heads(q_l, k_l, v_l)
    ql = apply_axial_rope(ql, cos, sin)
    kl = apply_axial_rope(kl, cos, sin)
    qc, kc, vc = heads(q_c, k_c, v_c)
    lv = lam_vec.astype(jnp.float32)
    lam = jnp.exp(jnp.sum(lv[0] * lv[1])) - jnp.exp(jnp.sum(lv[2] * lv[3])) + lam_init
    k_all = jnp.concatenate([kl, kc], axis=1)
    v_all = jnp.concatenate([vl, vc], axis=1)
    out_l = post(diff_attend_blocks(ql, k_all, v_all, lam))
    out_c = post(diff_attend(qc, kc, vc, lam)) if need_ctx else None
    return out_l, out_c


def rglru_coeffs(xc, gate_w, gate_b, lam):
    B, T, W = xc.shape
    xf = xc.astype(jnp.float32)
    xb = xf.reshape(B, T, LRU_BLOCKS, LRU_BLOCK_DIM)
    g = jnp.einsum('btnd,gnde->gbtne', xb, gate_w.astype(jnp.float32)).reshape(2, B, T, W)
    g = g + gate_b.astype(jnp.float32)[:, None, None, :]
    r = jax.nn.sigmoid(g[0])
    i = jax.nn.sigmoid(g[1])
    log_a = -LRU_C * r * jax.nn.softplus(-lam.astype(jnp.float32))
    a = jnp.exp(log_a)
    b = jnp.sqrt(-jnp.expm1(2.0 * log_a)) * (i * xf)
    return a, b


def linear_scan(a, b, h0):
    b = b.at[:, 0].add(a[:, 0] * h0)

    def combine(e1, e2):
        return e1[0] * e2[0], e2[0] * e1[1] + e2[1]

    _, h = lax.associative_scan(combine, (a, b), axis=1)
    return h


def rglru_branch(x_l, y_l, x_c, y_c, conv_w, conv_b, gate_w, gate_b, lam, need_ctx):
    x_l = dwconv_centred(x_l, conv_w, conv_b)
    x_c = dwconv_centred(x_c, conv_w, conv_b)
    h0 = jnp.zeros((x_c.shape[0], BRANCH_WIDTH), jnp.float32)
    h_l = 0.0
    h_c = 0.0
    for d in range(N_DIR):
        a_l, b_l = rglru_coeffs(x_l, gate_w[d], gate_b[d], lam[d])
        a_c, b_c = rglru_coeffs(x_c, gate_w[d], gate_b[d], lam[d])
        if d == 1:
            a_l, b_l, a_c, b_c = flip_t(a_l), flip_t(b_l), flip_t(a_c), flip_t(b_c)
        hc = linear_scan(a_c, b_c, h0)
        hl = linear_scan(a_l, b_l, hc[:, -1])
        if d == 1:
            hc, hl = flip_t(hc), flip_t(hl)
        h_l = h_l + hl
        h_c = h_c + hc
    out_l = jax.nn.gelu(y_l) * h_l.astype(y_l.dtype)
    out_c = jax.nn.gelu(y_c) * h_c.astype(y_c.dtype) if need_ctx else None
    return out_l, out_c


def gdn_chunked(q, k, v, g, beta, s0):
    B, T, H, dk = q.shape
    dv = v.shape[-1]
    C = GDN_CHUNK
    N = T // C
    f32 = jnp.float32

    def chunks(t):
        t = t.astype(f32).reshape((B, N, C, H) + t.shape[3:])
        return jnp.moveaxis(t, (1, 3), (0, 2))

    qc, kc, vc = chunks(q), chunks(k), chunks(v)
    gc = jnp.cumsum(chunks(g), axis=-1)
    bc = chunks(beta)
    idx = jnp.arange(C)
    incl = idx[:, None] >= idx[None, :]
    strict = (idx[:, None] > idx[None, :]).astype(f32)
    decay = jnp.exp(jnp.where(incl, gc[..., :, None] - gc[..., None, :], -jnp.inf))
    kb = kc * bc[..., None]
    lower = jnp.einsum('nbhid,nbhjd->nbhij', kb, kc) * decay * strict
    a_mat = lower + jnp.eye(C, dtype=f32)
    rhs = jnp.concatenate([vc * bc[..., None], kb * jnp.exp(gc)[..., None]], axis=-1)
    sol = lax.linalg.triangular_solve(a_mat, rhs, left_side=True, lower=True, unit_diagonal=True)
    u, w = sol[..., :dv], sol[..., dv:]
    attn = jnp.einsum('nbhid,nbhjd->nbhij', qc, kc) * decay
    q_dec = qc * jnp.exp(gc)[..., None]
    k_dec = kc * jnp.exp(gc[..., -1:] - gc)[..., None]
    g_end = jnp.exp(gc[..., -1])

    def step(s, xs):
        u_n, w_n, qd_n, kd_n, at_n, ge_n = xs
        v_new = u_n - jnp.einsum('bhcd,bhdv->bhcv', w_n, s)
        o_n = jnp.einsum('bhcd,bhdv->bhcv', qd_n, s) + jnp.einsum('bhij,bhjv->bhiv', at_n, v_new)
        s = s * ge_n[..., None, None] + jnp.einsum('bhcd,bhcv->bhdv', kd_n, v_new)
        return s, o_n

    s_fin, o = lax.scan(step, s0.astype(f32), (u, w, q_dec, k_dec, attn, g_end))
    o = jnp.moveaxis(o, (0, 2), (1, 3)).reshape(B, T, H, dv)
    return o, s_fin


def gdn_branch(qkv_l, z_l, b_l, a_l, qkv_c, z_c, b_c, a_c, conv_w, a_log, dt_bias, norm_g, need_ctx):
    def prep(qkv, b_raw, a_raw):
        B, T, _ = qkv.shape
        qkv = jax.nn.silu(dwconv_centred(qkv, conv_w))
        q, k, v = jnp.split(qkv, [GDN_HEADS * GDN_DK, 2 * GDN_HEADS * GDN_DK], axis=-1)
        q = l2_norm(q.reshape(B, T, GDN_HEADS, GDN_DK)) * (GDN_DK ** -0.5)
        k = l2_norm(k.reshape(B, T, GDN_HEADS, GDN_DK))
        v = v.reshape(B, T, GDN_HEADS, GDN_DV)
        b_raw = b_raw.astype(jnp.float32).reshape(B, T, N_DIR, GDN_HEADS)
        a_raw = a_raw.astype(jnp.float32).reshape(B, T, N_DIR, GDN_HEADS)
        return q, k, v, b_raw, a_raw

    def gates(b_raw, a_raw, d):
        rate = jnp.exp(a_log[d].astype(jnp.float32))
        g = -rate * jax.nn.softplus(a_raw[:, :, d] + dt_bias[d].astype(jnp.float32))
        return g, jax.nn.sigmoid(b_raw[:, :, d])

    def out(o, z):
        B, T = z.shape[:2]
        zh = z.astype(jnp.float32).reshape(B, T, GDN_HEADS, GDN_DV)
        return (rms_norm(o, norm_g) * jax.nn.silu(zh)).reshape(B, T, BRANCH_WIDTH).astype(z.dtype)

    ql, kl, vl, bl, al = prep(qkv_l, b_l, a_l)
    qc, kc, vc, bc, ac = prep(qkv_c, b_c, a_c)
    s0 = jnp.zeros((qc.shape[0], GDN_HEADS, GDN_DK, GDN_DV), jnp.float32)
    o_l = 0.0
    o_c = 0.0
    for d in range(N_DIR):
        g_l, beta_l = gates(bl, al, d)
        g_c, beta_c = gates(bc, ac, d)
        args_l = (ql, kl, vl, g_l, beta_l)
        args_c = (qc, kc, vc, g_c, beta_c)
        if d == 1:
            args_l = tuple(flip_t(t) for t in args_l)
            args_c = tuple(flip_t(t) for t in args_c)
        oc, s_ctx = gdn_chunked(*args_c, s0)
        ol, _ = gdn_chunked(*args_l, s_ctx)
        if d == 1:
            oc, ol = flip_t(oc), flip_t(ol)
        o_l = o_l + ol
        o_c = o_c + oc
    out_l = out(o_l, z_l)
    out_c = out(o_c, z_c) if need_ctx else None
    return out_l, out_c


def merge_branches(outs, gate_cols, w_branch, w_out):
    gates = jax.nn.sigmoid(gate_cols.astype(jnp.float32)).astype(gate_cols.dtype)
    gates = gates.reshape(gates.shape[:-1] + (N_BRANCH, D_MODEL))
    merged = sum(gates[..., i, :] * (outs[i] @ w_branch[i]) for i in range(N_BRANCH))
    return merged @ w_out


def sq_relu_mlp(h, w1, w2):
    return jnp.square(jax.nn.relu(h @ w1)) @ w2


def hybrid_layer(x, ctx, c_act, cctx_act, ada_w, ada_b, norm1_g, norm2_g, w_in,
                 da_q_norm_g, da_k_norm_g, da_lambda, da_sub_norm_g,
                 lru_conv_w, lru_conv_b, lru_gate_w, lru_gate_b, lru_lambda,
                 gdn_conv_w, gdn_A_log, gdn_dt_bias, gdn_norm_g,
                 w_branch, w_out, mlp_w1, mlp_w2, cos, sin, lam_init, need_ctx):
    mod_l = (c_act @ ada_w + ada_b)[:, None, :]
    mod_c = cctx_act @ ada_w + ada_b
    sh1_l, sc1_l, g1_l, sh2_l, sc2_l, g2_l = jnp.split(mod_l, N_MOD, axis=-1)
    sh1_c, sc1_c, g1_c, sh2_c, sc2_c, g2_c = jnp.split(mod_c, N_MOD, axis=-1)

    h_l = modulate(rms_norm(x, norm1_g), sh1_l, sc1_l)
    h_c = modulate(rms_norm(ctx, norm1_g), sh1_c, sc1_c)
    daq_l, dak_l, dav_l, lx_l, ly_l, gqkv_l, gz_l, gb_l, ga_l, gate_l = split_proj(h_l @ w_in)
    daq_c, dak_c, dav_c, lx_c, ly_c, gqkv_c, gz_c, gb_c, ga_c, gate_c = split_proj(h_c @ w_in)

    da_l, da_c = da_branch(daq_l, dak_l, dav_l, daq_c, dak_c, dav_c, da_q_norm_g, da_k_norm_g,
                           da_lambda, da_sub_norm_g, lam_init, cos, sin, need_ctx)
    lru_l, lru_c = rglru_branch(lx_l, ly_l, lx_c, ly_c, lru_conv_w, lru_conv_b, lru_gate_w,
                                lru_gate_b, lru_lambda, need_ctx)
    gdn_l, gdn_c = gdn_branch(gqkv_l, gz_l, gb_l, ga_l, gqkv_c, gz_c, gb_c, ga_c, gdn_conv_w,
                              gdn_A_log, gdn_dt_bias, gdn_norm_g, need_ctx)

    x = x + g1_l * merge_branches((da_l, lru_l, gdn_l), gate_l, w_branch, w_out)
    h2_l = modulate(rms_norm(x, norm2_g), sh2_l, sc2_l)
    x = x + g2_l * sq_relu_mlp(h2_l, mlp_w1, mlp_w2)
    if need_ctx:
        ctx = ctx + g1_c * merge_branches((da_c, lru_c, gdn_c), gate_c, w_branch, w_out)
        h2_c = modulate(rms_norm(ctx, norm2_g), sh2_c, sc2_c)
        ctx = ctx + g2_c * sq_relu_mlp(h2_c, mlp_w1, mlp_w2)
    return x, ctx


def setup_inputs(seed: int = 0) -> dict:
    key = jax.random.key(seed)
    ks = jax.random.split(key, 32)
    f32 = jnp.float32
    L = DEPTH

    def nrm(k, shape, scale):
        return jax.random.normal(k, shape, f32) * scale

    x = nrm(ks[0], (BATCH, SEQ, D_MODEL), 1.0)
    c = nrm(ks[1], (BATCH, D_MODEL), 1.0)
    ctx = nrm(ks[2], (BATCH, CTX_LEN, D_MODEL), 1.0)
    c_ctx = nrm(ks[3], (D_MODEL,), 1.0)
    ada_w = nrm(ks[4], (L, D_MODEL, N_MOD * D_MODEL), 0.5 * D_MODEL ** -0.5)
    ada_b = nrm(ks[5], (L, N_MOD * D_MODEL), 0.02)
    norm1_g = 1.0 + nrm(ks[6], (L, D_MODEL), 0.02)
    norm2_g = 1.0 + nrm(ks[7], (L, D_MODEL), 0.02)
    w_in = nrm(ks[8], (L, D_MODEL, N_IN), D_MODEL ** -0.5)
    da_q_norm_g = 1.0 + nrm(ks[9], (L, DA_QK_DIM), 0.02)
    da_k_norm_g = 1.0 + nrm(ks[10], (L, DA_QK_DIM), 0.02)
    da_lambda = nrm(ks[11], (L, 4, DA_QK_DIM), 0.1)
    da_sub_norm_g = 1.0 + nrm(ks[12], (L, DA_V_DIM), 0.02)
    lru_conv_w = nrm(ks[13], (L, CONV_K, BRANCH_WIDTH), CONV_K ** -0.5)
    lru_conv_b = nrm(ks[14], (L, BRANCH_WIDTH), 0.02)
    lru_gate_w = nrm(ks[15], (L, N_DIR, 2, LRU_BLOCKS, LRU_BLOCK_DIM, LRU_BLOCK_DIM), LRU_BLOCK_DIM ** -0.5)
    lru_gate_b = nrm(ks[16], (L, N_DIR, 2, BRANCH_WIDTH), 0.1)
    a0 = jax.random.uniform(ks[17], (L, N_DIR, BRANCH_WIDTH), f32, 0.9, 0.999) ** (1.0 / LRU_C)
    lru_lambda = jnp.log(a0) - jnp.log1p(-a0)
    gdn_conv_w = nrm(ks[18], (L, CONV_K, GDN_QKV_WIDTH), CONV_K ** -0.5)
    gdn_A_log = jnp.log(jax.random.uniform(ks[19], (L, N_DIR, GDN_HEADS), f32, 1.0, 16.0))
    dt = jnp.exp(jax.random.uniform(ks[20], (L, N_DIR, GDN_HEADS), f32, math.log(1e-3), math.log(1e-1)))
    gdn_dt_bias = dt + jnp.log(-jnp.expm1(-dt))
    gdn_norm_g = 1.0 + nrm(ks[21], (L, GDN_DV), 0.02)
    w_branch = nrm(ks[22], (L, N_BRANCH, BRANCH_WIDTH, D_MODEL), BRANCH_WIDTH ** -0.5)
    w_out = nrm(ks[23], (L, D_MODEL, D_MODEL), D_MODEL ** -0.5)
    mlp_w1 = nrm(ks[24], (L, D_MODEL, D_FF), D_MODEL ** -0.5)
    mlp_w2 = nrm(ks[25], (L, D_FF, D_MODEL), D_FF ** -0.5)
    return {'x': x, 'c': c, 'ctx': ctx, 'c_ctx': c_ctx, 'ada_w': ada_w, 'ada_b': ada_b,
            'norm1_g': norm1_g, 'norm2_g': norm2_g, 'w_in': w_in,
            'da_q_norm_g': da_q_norm_g, 'da_k_norm_g': da_k_norm_g, 'da_lambda': da_lambda,
            'da_sub_norm_g': da_sub_norm_g, 'lru_conv_w': lru_conv_w, 'lru_conv_b': lru_conv_b,
            'lru_gate_w': lru_gate_w, 'lru_gate_b': lru_gate_b, 'lru_lambda': lru_lambda,
            'gdn_conv_w': gdn_conv_w, 'gdn_A_log': gdn_A_log, 'gdn_dt_bias': gdn_dt_bias,
            'gdn_norm_g': gdn_norm_g, 'w_branch': w_branch, 'w_out': w_out,
            'mlp_w1': mlp_w1, 'mlp_w2': mlp_w2}


def reference(x, c, ctx, c_ctx, ada_w, ada_b, norm1_g, norm2_g, w_in,
              da_q_norm_g, da_k_norm_g, da_lambda, da_sub_norm_g,
              lru_conv_w, lru_conv_b, lru_gate_w, lru_gate_b, lru_lambda,
              gdn_conv_w, gdn_A_log, gdn_dt_bias, gdn_norm_g,
              w_branch, w_out, mlp_w1, mlp_w2):
    ROWS = x.shape[1] // GRID_W
    row = jnp.repeat(jnp.arange(ROWS, dtype=jnp.int32), GRID_W)
    col = jnp.tile(jnp.arange(GRID_W, dtype=jnp.int32), ROWS)
    cos, sin = axial_rope_tables(row, col)
    c_act = jax.nn.silu(c)
    cctx_act = jax.nn.silu(c_ctx)
    for layer in range(DEPTH):
        lam_init = 0.8 - 0.6 * math.exp(-0.3 * layer)
        x, ctx = hybrid_layer(x, ctx, c_act, cctx_act, ada_w[layer], ada_b[layer],
                              norm1_g[layer], norm2_g[layer], w_in[layer],
                              da_q_norm_g[layer], da_k_norm_g[layer], da_lambda[layer], da_sub_norm_g[layer],
                              lru_conv_w[layer], lru_conv_b[layer], lru_gate_w[layer], lru_gate_b[layer],
                              lru_lambda[layer], gdn_conv_w[layer], gdn_A_log[layer], gdn_dt_bias[layer],
                              gdn_norm_g[layer], w_branch[layer], w_out[layer], mlp_w1[layer], mlp_w2[layer],
                              cos, sin, lam_init, layer < DEPTH - 1)
    return x
```

```python
import contextlib
import math
import numpy as np
import concourse.bass as bass
import concourse.mybir as mybir
from concourse.bass_utils import run_bass_kernel_spmd

F32 = mybir.dt.float32
BF16 = mybir.dt.bfloat16
AF = mybir.ActivationFunctionType
ALU = mybir.AluOpType
AX = mybir.AxisListType

D = 1024
DEPTH = 2
GRID_W = 64
NIN = 7696
DFF = 4096
EPS = 1e-6
NEG = -30000.0


class Res:
    __slots__ = ("w", "r", "g")

    def __init__(self):
        self.w = {}
        self.r = {}
        self.g = None


class Eng:
    def __init__(self, fw, e, name, own_sync=True):
        self.fw = fw
        self.e = e
        self.name = name
        self.sem = fw.es.enter_context(fw.nc.semaphore("s_" + name))
        self.count = 0
        self.waited = {}
        self.own_sync = own_sync
        self.dma_sems = []
        self.dma_i = 0

    def _wait(self, ev):
        sem, val = ev
        key = id(sem)
        if self.waited.get(key, 0) >= val:
            return
        if sem is self.sem and not self.own_sync:
            return
        self.e.wait_ge(sem, val)
        self.waited[key] = val

    def _deps(self, reads, writes, par):
        for r in reads:
            for ev in r.w.values():
                self._wait(ev)
        for w in writes:
            if not par:
                for ev in w.w.values():
                    self._wait(ev)
            elif w.g is not None:
                self._wait(w.g)
            for ev in w.r.values():
                self._wait(ev)

    def _commit(self, ev, reads, writes, par):
        k = id(ev[0])
        for r in reads:
            old = r.r.get(k)
            if old is None or old[1] < ev[1]:
                r.r[k] = ev
        for w in writes:
            if par:
                old = w.w.get(k)
                if old is None or old[1] < ev[1]:
                    w.w[k] = ev
            else:
                w.w = {k: ev}
                w.r = {}
                w.g = ev

    def op(self, fn, reads=(), writes=(), par=False):
        self._deps(reads, writes, par)
        ins = fn(self.e)
        self.count += 1
        ins.then_inc(self.sem, 1)
        ev = (self.sem, self.count)
        self._commit(ev, reads, writes, par)
        return ev

    def dma(self, out, in_, reads=(), writes=(), par=False, **kw):
        K = len(self.dma_sems)
        i = self.dma_i
        self.dma_i += 1
        sem = self.dma_sems[i % K]
        rnd = i // K
        if rnd > 0:
            self._wait((sem, 16 * rnd))
        self._deps(reads, writes, par)
        self.e.dma_start(out=out, in_=in_, **kw).then_inc(sem, 16)
        ev = (sem, 16 * (rnd + 1))
        self._commit(ev, reads, writes, par)
        return ev


class FW:
    def __init__(self, nc, es, ndma=12):
        self.nc = nc
        self.es = es
        self.pe = Eng(self, nc.tensor, "pe", own_sync=False)
        self.act = Eng(self, nc.scalar, "act")
        self.dve = Eng(self, nc.vector, "dve")
        self.pool = Eng(self, nc.gpsimd, "pool")
        self.sp = Eng(self, nc.sync, "sp")
        for q, n in ((self.sp, ndma), (self.pool, ndma)):
            q.dma_sems = [es.enter_context(nc.semaphore(f"d_{q.name}{i}")) for i in range(n)]
        self.engs = [self.pe, self.act, self.dve, self.pool, self.sp]
        self._rr = 0

    def barrier(self):
        evs = []
        for e in self.engs:
            if e.count > 0:
                evs.append((e.sem, e.count))
        for q in (self.sp, self.pool):
            K = len(q.dma_sems)
            for i, s in enumerate(q.dma_sems):
                n_on = (q.dma_i - i + K - 1) // K
                if n_on > 0:
                    evs.append((s, 16 * n_on))
        for e in self.engs:
            for ev in evs:
                e._wait(ev)

    def finish(self):
        for e in self.engs:
            if e.count > 0:
                self.sp._wait((e.sem, e.count))
        for q in (self.sp, self.pool):
            K = len(q.dma_sems)
            for i, s in enumerate(q.dma_sems):
                n_on = (q.dma_i - i + K - 1) // K
                if n_on > 0:
                    self.sp._wait((s, 16 * n_on))


class Buf:
    def __init__(self, t):
        self.t = t
        self.res = Res()


class Ring:
    def __init__(self, bufs):
        self.bufs = bufs
        self.i = 0

    def next(self):
        b = self.bufs[self.i % len(self.bufs)]
        self.i += 1
        return b


class Scope:
    def __init__(self, fw):
        self.fw = fw
        self.es = contextlib.ExitStack()
        self.n = 0

    def __enter__(self):
        self.es.__enter__()
        return self

    def __exit__(self, *a):
        if a[0] is None:
            self.fw.barrier()
        return self.es.__exit__(*a)

    def sb(self, name, shape, dt=F32):
        return Buf(self.es.enter_context(self.fw.nc.sbuf_tensor(name + f"_{id(self) % 9973}", list(shape), dt)))

    def ps(self, name, shape, dt=F32):
        return Buf(self.es.enter_context(self.fw.nc.psum_tensor(name + f"_{id(self) % 9973}", list(shape), dt)))

    def sbring(self, name, shape, dt, n):
        return Ring([self.sb(f"{name}{i}", shape, dt) for i in range(n)])

    def psring(self, name, shape, dt, n):
        return Ring([self.ps(f"{name}{i}", shape, dt) for i in range(n)])


class Cfg:
    def __init__(self, T=8192, TC=256, depth=DEPTH, debug=False):
        self.T = T
        self.TC = TC
        self.H = T // 2
        self.S = TC + T
        self.NO = TC + T // 2
        self.depth = depth
        self.debug = debug

    def blocks(self, lo, hi, tb=512):
        out = []
        segs = [(0, self.TC), (self.TC, self.NO), (self.NO, self.S)]
        for a, b in segs:
            a2, b2 = max(a, lo), min(b, hi)
            t = a2
            while t < b2:
                n = min(tb, b2 - t)
                out.append((t, n))
                t += n
        return out


O_Q, O_K, O_V, O_LX, O_LY, O_GQKV, O_GZ, O_GB, O_GA, O_GATE = 0, 512, 1024, 1536, 2048, 2560, 4096, 4608, 4616, 4624
P_K, P_LX, P_G, P_V, P_GBA, P_Q, P_LY, P_GZ, P_GATE = 0, 512, 1024, 2560, 3072, 3088, 3600, 4112, 4624
NA = 3088
NB = NIN - NA

PV_N1G = 0
PV_N2G = 8
PV_ADAB = 16
PV_QG = 64
PV_KG = 65
PV_SUBG = 66
PV_LAM = 67
PV_LCW = 71
PV_LCB = 91
PV_LGB = 95
PV_LLAM = 111
PV_GCW = 119
PV_GNG = 179
PV_ALOG = 180
PV_DTB = 188
NPV = 196


class Prog:
    def __init__(self, cfg, lbase=0, final=True):
        self.cfg = cfg
        self.lbase = lbase
        self.final = final
        nc = self.nc = bass.Bass("TRN2", target_bir_lowering=False)
        T, TC, S, NO, H, L = cfg.T, cfg.TC, cfg.S, cfg.NO, cfg.H, cfg.depth
        self.L = L

        def din(name, shape, dt=F32):
            return nc.dram_tensor(name, list(shape), dt, kind="ExternalInput").ap()

        def dscr(name, shape, dt=F32):
            if cfg.debug:
                return nc.dram_tensor(name, list(shape), dt, kind="ExternalOutput").ap()
            return nc.dram_tensor(name, list(shape), dt).ap()

        self.x_loc = din("x_loc", [S, D])
        self.cvec = din("cvec", [128, 8, 2])
        self.ada_w = din("ada_w", [L, D, 6 * D])
        self.pvtab = din("pvtab", [L, 128, NPV])
        self.w_in = din("w_in", [L, D, NIN])
        self.lru_gw = din("lru_gw", [L, 2, 2, 4, 128, 128])
        self.w_branch = din("w_branch", [L, 3, 512, D])
        self.w_out = din("w_out", [L, D, D])
        self.w1 = din("w1", [L, D, DFF])
        self.w2 = din("w2", [L, DFF, D])
        self.ropec = din("ropec", [128, S])
        self.ropes = din("ropes", [128, S])
        self.consts = din("consts", [128, 10, 128])
        self.out_loc = nc.dram_tensor("out_loc", [H if final else NO, D], F32, kind="ExternalOutput").ap()
        self.w_in_bf = nc.dram_tensor("w_in_bf", [L, D, NIN], BF16).ap()
        self.wbr_bf = nc.dram_tensor("wbr_bf", [L, 3, 512, D], BF16).ap()
        self.wout_bf = nc.dram_tensor("wout_bf", [L, D, D], BF16).ap()
        self.w1_bf = nc.dram_tensor("w1_bf", [L, D, DFF], BF16).ap()
        self.w2_bf = nc.dram_tensor("w2_bf", [L, DFF, D], BF16).ap()
        self.xT = dscr("xT", [D, S])
        self.qT = dscr("qT", [512, NO], BF16)
        self.kT = dscr("kT", [512, S], BF16)
        self.v_tok = dscr("v_tok", [S, 512], BF16)
        self.lxT = dscr("lxT", [512, S])
        self.lyT = dscr("lyT", [512, NO])
        self.gT = dscr("gT", [1536, S])
        self.gzT = dscr("gzT", [512, NO])
        self.gateT = dscr("gateT", [3 * D, NO], BF16)
        self.gpar = dscr("gpar", [S, 16])
        self.oT = dscr("oT", [3, 512, NO], BF16)
        self.R = {}

    def dres(self, name):
        if name not in self.R:
            self.R[name] = Res()
        return self.R[name]

    def build(self, stop_after=None):
        with contextlib.ExitStack() as es:
            self.fw = FW(self.nc, es)
            self.setup(es)
            self.p0_mod()
            self.p0_xT()
            for l in range(self.L):
                self.p1(l)
                if stop_after == "p1":
                    break
                self.lru(l)
                if stop_after == "lru":
                    break
                self.da(l)
                if stop_after == "da":
                    break
                self.gdn(l)
                if stop_after == "gdn":
                    break
                self.p3a(l, last=(self.final and l == self.L - 1))
                self.p3b(l, last=(self.final and l == self.L - 1), emit=(l == self.L - 1))
                if stop_after == "l0":
                    break
            self.fw.finish()
        return self.nc

    def setup(self, es):
        fw, L = self.fw, self.L
        pe, act, dve, pool, sp = fw.pe, fw.act, fw.dve, fw.pool, fw.sp
        G = Scope(fw)
        es.enter_context(G.es)
        self.G = G
        cst = self.cst = G.sb("cst", [128, 10, 128])
        (self.ident, self.ones, self.blk64, self.rotm, self.triu, self.negu, self.negl, self.tril, self.stru, self.strl) = \
            [cst.t[:, i, :] for i in range(10)]
        sp.dma(cst.t[:], self.consts[:, :, :], writes=[cst.res])
        self.identb = G.sb("identb", [128, 128], BF16)
        dve.op(lambda e: e.tensor_copy(self.identb.t[:], self.ident), reads=[cst.res], writes=[self.identb.res])
        self.onesb = G.sb("onesb", [128, 128], BF16)
        dve.op(lambda e: e.tensor_copy(self.onesb.t[:], self.ones), reads=[cst.res], writes=[self.onesb.res])
        pv = self.pv = G.sb("pv", [128, L, NPV])
        for l in range(L):
            sp.dma(pv.t[:, l, :], self.pvtab[l, :, :], writes=[pv.res], par=True)
        self.modv = G.sb("modv", [128, L, 48, 2])
        self.A1 = G.sb("A1", [128, L, 8, 2])
        self.A2 = G.sb("A2", [128, L, 8, 2])
        self.epsb = G.sb("epsb", [128, 1])
        dve.op(lambda e: e.memset(self.epsb.t[:], EPS), writes=[self.epsb.res])
        self.lnq = G.sb("lnq", [128, 1])
        dve.op(lambda e: e.memset(self.lnq.t[:], -0.5 * math.log(128.0)), writes=[self.lnq.res])
        self.zerob = G.sb("zerob", [128, 1])
        dve.op(lambda e: e.memset(self.zerob.t[:], 0.0), writes=[self.zerob.res])
        self.gpt = G.sb("gpt", [128, self.cfg.S // 128, 16])

        def cast_rows(dst, src, nrows, res):
            for r0 in range(0, nrows, 128):
                pool.dma(dst[r0:r0 + 128, :], src[r0:r0 + 128, :], writes=[res], par=True)

        for l in range(L):
            cast_rows(self.w_in_bf[l], self.w_in[l], D, self.dres(f"w_in{l}"))
            for i in range(3):
                cast_rows(self.wbr_bf[l, i], self.w_branch[l, i], 512, self.dres(f"w3a{l}"))
            cast_rows(self.wout_bf[l], self.w_out[l], D, self.dres(f"w3a{l}"))
            cast_rows(self.w1_bf[l], self.w1[l], D, self.dres(f"w3b{l}"))
            cast_rows(self.w2_bf[l], self.w2[l], DFF, self.dres(f"w3b{l}"))

    def MV(self, l, which, j, w):
        return self.modv.t[:, l, which * 8 + j, w:w + 1]

    def PV(self, l, col):
        return self.pv.t[:, l, col:col + 1]

    def p0_mod(self):
        fw, L = self.fw, self.L
        pe, act, dve, pool, sp = fw.pe, fw.act, fw.dve, fw.pool, fw.sp
        pv, modv = self.pv, self.modv
        with Scope(fw) as sc:
            cv = sc.sb("cv", [128, 8, 2])
            sp.dma(cv.t[:], self.cvec[:, :, :], writes=[cv.res])
            scv = sc.sb("scv", [128, 8, 2])
            act.op(lambda e: e.activation(out=scv.t[:], in_=cv.t[:], func=AF.Silu), reads=[cv.res], writes=[scv.res])
            wring = sc.sbring("adaw", [128, 8, 512], F32, 2)
            mps = sc.ps("modps", [128, 48, 2])
            for l in range(L):
                for gi in range(12):
                    wb = wring.next()
                    sp.dma(wb.t[:], self.ada_w[l, :, gi * 512:(gi + 1) * 512].rearrange("(kc p) n -> p kc n", p=128),
                           writes=[wb.res])
                    for j in range(4):
                        fc = gi * 4 + j
                        for k in range(8):
                            pe.op(lambda e: e.matmul(mps.t[:, fc, :], wb.t[:, k, j * 128:(j + 1) * 128], scv.t[:, k, :],
                                                     start=(k == 0), stop=(k == 7)),
                                  reads=[wb.res, scv.res], writes=[mps.res])
                for w in range(2):
                    dve.op(lambda e: e.tensor_tensor(modv.t[:, l, :, w], mps.t[:, :, w], pv.t[:, l, PV_ADAB:PV_ADAB + 48], ALU.add),
                           reads=[mps.res, pv.res], writes=[modv.res], par=(w == 1))
                for (Ax, sc0, g0) in ((self.A1, 8, PV_N1G), (self.A2, 32, PV_N2G)):
                    for w in range(2):
                        dve.op(lambda e: e.scalar_tensor_tensor(Ax.t[:, l, :, w], modv.t[:, l, sc0:sc0 + 8, w], 1.0,
                                                                pv.t[:, l, g0:g0 + 8], ALU.add, ALU.mult),
                               reads=[modv.res, pv.res], writes=[Ax.res])

    def p0_xT(self):
        fw, S = self.fw, self.cfg.S
        pe, act, dve, pool, sp = fw.pe, fw.act, fw.dve, fw.pool, fw.sp
        with Scope(fw) as sc:
            xin = sc.sbring("xin", [128, D], F32, 2)
            tps = sc.psring("tps", [128, 512], F32, 4)
            stg = sc.sbring("xstg", [128, 8, 128], F32, 2)
            for ti in range(S // 128):
                xb = xin.next()
                sp.dma(xb.t[:], self.x_loc[ti * 128:(ti + 1) * 128, :], writes=[xb.res])
                st = stg.next()
                for half in range(2):
                    tp = tps.next()
                    for j in range(4):
                        c = half * 4 + j
                        pe.op(lambda e: e.transpose(tp.t[:, j * 128:(j + 1) * 128], xb.t[:, c * 128:(c + 1) * 128], self.ident),
                              reads=[xb.res, self.cst.res], writes=[tp.res])
                    if half == 0:
                        act.op(lambda e: e.copy(st.t[:, 0:4, :], tp.t[:].rearrange("p (c n) -> p c n", c=4)),
                               reads=[tp.res], writes=[st.res])
                    else:
                        dve.op(lambda e: e.tensor_copy(st.t[:, 4:8, :], tp.t[:].rearrange("p (c n) -> p c n", c=4)),
                               reads=[tp.res], writes=[st.res], par=True)
                sp.dma(self.xT[:, ti * 128:(ti + 1) * 128].rearrange("(c p) n -> p c n", p=128), st.t[:],
                       reads=[st.res], writes=[self.dres("xT")], par=True)

    def norm_mod(self, sc, l, xb, n, is_ctx, hT, Amod, sh_which, tools):
        fw = self.fw
        pe, act, dve, pool, sp = fw.pe, fw.act, fw.dve, fw.pool, fw.sp
        sqring, tmpring, ss, rstd = tools
        w = 1 if is_ctx else 0
        for c in range(8):
            sq = sqring.next()
            act.op(lambda e: e.activation(out=sq.t[:, :n], in_=xb.t[:, c, :n], func=AF.Square),
                   reads=[xb.res], writes=[sq.res])
            pe.op(lambda e: e.matmul(ss.t[:, :n], self.ones, sq.t[:, :n], start=(c == 0), stop=(c == 7)),
                  reads=[sq.res, self.cst.res], writes=[ss.res])
        act.op(lambda e: e.activation(out=rstd.t[:, :n], in_=ss.t[:, :n], func=AF.Sqrt, scale=1.0 / D,
                                      bias=self.epsb.t[:, 0:1]),
               reads=[ss.res, self.epsb.res], writes=[rstd.res])
        dve.op(lambda e: e.reciprocal(rstd.t[:, :n], rstd.t[:, :n]), reads=[rstd.res], writes=[rstd.res])
        for c in range(8):
            tmp = tmpring.next()
            eng = dve if c % 2 == 0 else pool
            eng.op(lambda e: e.tensor_tensor(tmp.t[:, :n], xb.t[:, c, :n], rstd.t[:, :n], ALU.mult),
                   reads=[xb.res, rstd.res], writes=[tmp.res])
            act.op(lambda e: e.activation(out=hT.t[:, c, :n], in_=tmp.t[:, :n], func=AF.Identity,
                                          scale=Amod.t[:, l, c, w:w + 1], bias=self.MV(l, sh_which, c, w)),
                   reads=[tmp.res, Amod.res, self.modv.res], writes=[hT.res], par=(c > 0))

    def p1(self, l):
        cfg, fw = self.cfg, self.fw
        pe, act, dve, pool, sp = fw.pe, fw.act, fw.dve, fw.pool, fw.sp
        S, NO, TC = cfg.S, cfg.NO, cfg.TC
        cst = self.cst
        for pss in range(2):
            ncol = NA if pss == 0 else NB
            c0 = 0 if pss == 0 else NA
            hi = S if pss == 0 else NO
            with Scope(fw) as sc:
                W = sc.sb("Win", [128, 8, ncol], BF16)
                for k in range(8):
                    sp.dma(W.t[:, k, :], self.w_in_bf[l, k * 128:(k + 1) * 128, c0:c0 + ncol],
                           reads=[self.dres(f"w_in{l}")], writes=[W.res], par=True)
                xring = sc.sbring("xb", [128, 8, 512], F32, 2)
                hring = sc.sbring("hT", [128, 8, 512], BF16, 2)
                tools = (sc.sbring("sq", [128, 512], F32, 2), sc.sbring("nt", [128, 512], F32, 2),
                         sc.ps("ss", [128, 512]), sc.sb("rstd", [128, 512]))
                pj = sc.psring("pj", [128, 512], F32, 3)
                aux = sc.psring("aux", [128, 512], F32, 2)
                pv_ps = sc.psring("pvps", [128, 512], F32, 2)
                stg32 = sc.sbring("s32", [128, 512], F32, 4)
                stg16 = sc.sbring("s16", [128, 512], BF16, 4)
                t32 = sc.sbring("t32", [128, 512], F32, 6)
                cosr = sc.sbring("cos", [128, 512], F32, 2)
                sinr = sc.sbring("sin", [128, 512], F32, 2)
                flip = [0]

                def evac_engine():
                    flip[0] ^= 1
                    return act if flip[0] else dve

                def proj_fm(hT, col, n):
                    p = pj.next()
                    for k in range(8):
                        pe.op(lambda e: e.matmul(p.t[:, :n], W.t[:, k, col:col + 128], hT.t[:, k, :n],
                                                 start=(k == 0), stop=(k == 7)),
                              reads=[W.res, hT.res], writes=[p.res])
                    return p

                def store(dst, st, n, name):
                    sp.dma(dst, st.t[:, :n], reads=[st.res], writes=[self.dres(name)], par=True)

                def raw_store(hT, col, n, dst, name):
                    p = proj_fm(hT, col, n)
                    st = stg32.next()
                    if evac_engine() is act:
                        act.op(lambda e: e.copy(st.t[:, :n], p.t[:, :n]), reads=[p.res], writes=[st.res])
                    else:
                        dve.op(lambda e: e.tensor_copy(st.t[:, :n], p.t[:, :n]), reads=[p.res], writes=[st.res])
                    store(dst, st, n, name)

                def qk_store(hT, col, n, t0, gcol, dst, name, cs, sn):
                    p = proj_fm(hT, col, n)
                    sq = t32.next()
                    act.op(lambda e: e.activation(out=sq.t[:, :n], in_=p.t[:, :n], func=AF.Square),
                           reads=[p.res], writes=[sq.res])
                    a1 = aux.next()
                    pe.op(lambda e: e.matmul(a1.t[:, :n], self.blk64, sq.t[:, :n], start=True, stop=True),
                          reads=[sq.res, cst.res], writes=[a1.res])
                    rq = t32.next()
                    act.op(lambda e: e.activation(out=rq.t[:, :n], in_=a1.t[:, :n], func=AF.Sqrt, scale=1.0 / 64,
                                                  bias=self.epsb.t[:, 0:1]),
                           reads=[a1.res, self.epsb.res], writes=[rq.res])
                    dve.op(lambda e: e.reciprocal(rq.t[:, :n], rq.t[:, :n]), reads=[rq.res], writes=[rq.res])
                    xn = t32.next()
                    dve.op(lambda e: e.scalar_tensor_tensor(xn.t[:, :n], p.t[:, :n], self.PV(l, gcol), rq.t[:, :n],
                                                            ALU.mult, ALU.mult),
                           reads=[p.res, rq.res, self.pv.res], writes=[xn.res])
                    a2 = aux.next()
                    pe.op(lambda e: e.matmul(a2.t[:, :n], self.rotm, xn.t[:, :n], start=True, stop=True),
                          reads=[xn.res, cst.res], writes=[a2.res])
                    t1 = t32.next()
                    pool.op(lambda e: e.tensor_tensor(t1.t[:, :n], xn.t[:, :n], cs.t[:, :n], ALU.mult),
                            reads=[xn.res, cs.res], writes=[t1.res])
                    t2 = t32.next()
                    dve.op(lambda e: e.tensor_tensor(t2.t[:, :n], a2.t[:, :n], sn.t[:, :n], ALU.mult),
                           reads=[a2.res, sn.res], writes=[t2.res])
                    st = stg16.next()
                    pool.op(lambda e: e.tensor_tensor(st.t[:, :n], t1.t[:, :n], t2.t[:, :n], ALU.add),
                            reads=[t1.res, t2.res], writes=[st.res])
                    store(dst, st, n, name)

                for (t0, n) in cfg.blocks(0, hi):
                    is_ctx = t0 < TC
                    xb = xring.next()
                    sp.dma(xb.t[:, :, :n], self.xT[:, t0:t0 + n].rearrange("(c p) n -> p c n", p=128),
                           reads=[self.dres("xT")], writes=[xb.res])
                    hT = hring.next()
                    self.norm_mod(sc, l, xb, n, is_ctx, hT, self.A1, 0, tools)
                    if pss == 0:
                        cs = cosr.next(); sn = sinr.next()
                        sp.dma(cs.t[:, :n], self.ropec[:, t0:t0 + n], writes=[cs.res])
                        sp.dma(sn.t[:, :n], self.ropes[:, t0:t0 + n], writes=[sn.res])
                        for j in range(4):
                            qk_store(hT, P_K + j * 128, n, t0, PV_KG, self.kT[j * 128:(j + 1) * 128, t0:t0 + n], "kT", cs, sn)
                        for j in range(4):
                            raw_store(hT, P_LX + j * 128, n, self.lxT[j * 128:(j + 1) * 128, t0:t0 + n], "lxT")
                        for j in range(12):
                            raw_store(hT, P_G + j * 128, n, self.gT[j * 128:(j + 1) * 128, t0:t0 + n], "gT")
                        for sub in range(n // 128):
                            tt = t0 + sub * 128
                            p = pv_ps.next()
                            for k in range(8):
                                pe.op(lambda e: e.matmul(p.t[:, :], hT.t[:, k, sub * 128:(sub + 1) * 128], W.t[:, k, P_V:P_V + 512],
                                                         start=(k == 0), stop=(k == 7)),
                                      reads=[W.res, hT.res], writes=[p.res])
                            st = stg16.next()
                            act.op(lambda e: e.copy(st.t[:, :], p.t[:, :]), reads=[p.res], writes=[st.res])
                            sp.dma(self.v_tok[tt:tt + 128, :], st.t[:, :], reads=[st.res], writes=[self.dres("v_tok")], par=True)
                            p2 = aux.next()
                            for k in range(8):
                                pe.op(lambda e: e.matmul(p2.t[:, 0:16], hT.t[:, k, sub * 128:(sub + 1) * 128], W.t[:, k, P_GBA:P_GBA + 16],
                                                         start=(k == 0), stop=(k == 7)),
                                      reads=[W.res, hT.res], writes=[p2.res])
                            dve.op(lambda e: e.tensor_copy(self.gpt.t[:, tt // 128, :], p2.t[:, 0:16]),
                                   reads=[p2.res], writes=[self.gpt.res], par=True)
                    else:
                        cs = cosr.next(); sn = sinr.next()
                        sp.dma(cs.t[:, :n], self.ropec[:, t0:t0 + n], writes=[cs.res])
                        sp.dma(sn.t[:, :n], self.ropes[:, t0:t0 + n], writes=[sn.res])
                        for j in range(4):
                            qk_store(hT, P_Q - NA + j * 128, n, t0, PV_QG, self.qT[j * 128:(j + 1) * 128, t0:t0 + n], "qT", cs, sn)
                        for j in range(4):
                            p = proj_fm(hT, P_LY - NA + j * 128, n)
                            x2 = t32.next()
                            act.op(lambda e: e.activation(out=x2.t[:, :n], in_=p.t[:, :n], func=AF.Square),
                                   reads=[p.res], writes=[x2.res])
                            dve.op(lambda e: e.tensor_scalar(x2.t[:, :n], x2.t[:, :n], 0.044715, 1.0, ALU.mult, ALU.add),
                                   reads=[x2.res], writes=[x2.res])
                            dve.op(lambda e: e.tensor_tensor(x2.t[:, :n], x2.t[:, :n], p.t[:, :n], ALU.mult),
                                   reads=[x2.res, p.res], writes=[x2.res])
                            act.op(lambda e: e.activation(out=x2.t[:, :n], in_=x2.t[:, :n], func=AF.Sigmoid,
                                                          scale=2.0 * math.sqrt(2.0 / math.pi)),
                                   reads=[x2.res], writes=[x2.res])
                            st = stg32.next()
                            dve.op(lambda e: e.tensor_tensor(st.t[:, :n], x2.t[:, :n], p.t[:, :n], ALU.mult),
                                   reads=[x2.res, p.res], writes=[st.res])
                            store(self.lyT[j * 128:(j + 1) * 128, t0:t0 + n], st, n, "lyT")
                        for j in range(4):
                            p = proj_fm(hT, P_GZ - NA + j * 128, n)
                            st = stg32.next()
                            act.op(lambda e: e.activation(out=st.t[:, :n], in_=p.t[:, :n], func=AF.Silu),
                                   reads=[p.res], writes=[st.res])
                            store(self.gzT[j * 128:(j + 1) * 128, t0:t0 + n], st, n, "gzT")
                        for j in range(24):
                            p = proj_fm(hT, P_GATE - NA + j * 128, n)
                            st = stg16.next()
                            act.op(lambda e: e.activation(out=st.t[:, :n], in_=p.t[:, :n], func=AF.Sigmoid),
                                   reads=[p.res], writes=[st.res])
                            store(self.gateT[j * 128:(j + 1) * 128, t0:t0 + n], st, n, "gateT")
        self.gdn_params(l)

    def gdn_params(self, l):
        cfg, fw = self.cfg, self.fw
        pe, act, dve, pool, sp = fw.pe, fw.act, fw.dve, fw.pool, fw.sp
        NT = cfg.S // 128
        gpt, pv = self.gpt, self.pv
        with Scope(fw) as sc:
            gb = gpt.t[:, :, 0:8]
            ga = gpt.t[:, :, 8:16]
            y = sc.sb("y", [128, NT, 8]); ln = sc.sb("ln", [128, NT, 8]); po = sc.sb("po", [128, NT, 8])
            msk = sc.sb("msk", [128, NT, 8]); rate = sc.sb("rate", [128, 8])
            act.op(lambda e: e.activation(out=gb, in_=gb, func=AF.Sigmoid), reads=[gpt.res], writes=[gpt.res])
            dtb = pv.t[:, l, PV_DTB:PV_DTB + 8]
            dve.op(lambda e: e.tensor_tensor(ga, ga, dtb.unsqueeze(1).to_broadcast([128, NT, 8]), ALU.add),
                   reads=[gpt.res, pv.res], writes=[gpt.res])
            act.op(lambda e: e.activation(out=y.t[:], in_=ga, func=AF.Exp), reads=[gpt.res], writes=[y.res])
            act.op(lambda e: e.activation(out=rate.t[:], in_=pv.t[:, l, PV_ALOG:PV_ALOG + 8], func=AF.Exp),
                   reads=[pv.res], writes=[rate.res])
            act.op(lambda e: e.activation(out=ln.t[:], in_=y.t[:], func=AF.Ln, bias=1.0, scale=1.0),
                   reads=[y.res], writes=[ln.res])
            dve.op(lambda e: e.tensor_scalar(po.t[:], y.t[:], 0.2, -0.25, ALU.mult, ALU.add), reads=[y.res], writes=[po.res])
            for cf in (1.0 / 3, -0.5, 1.0):
                dve.op(lambda e: e.tensor_tensor(po.t[:], po.t[:], y.t[:], ALU.mult), reads=[po.res, y.res], writes=[po.res])
                dve.op(lambda e: e.tensor_scalar(po.t[:], po.t[:], cf, None, ALU.add), reads=[po.res], writes=[po.res])
            dve.op(lambda e: e.tensor_tensor(po.t[:], po.t[:], y.t[:], ALU.mult), reads=[po.res, y.res], writes=[po.res])
            dve.op(lambda e: e.tensor_scalar(msk.t[:], y.t[:], 0.125, None, ALU.is_lt), reads=[y.res], writes=[msk.res])
            dve.op(lambda e: e.tensor_tensor(po.t[:], po.t[:], ln.t[:], ALU.subtract), reads=[po.res, ln.res], writes=[po.res])
            dve.op(lambda e: e.tensor_tensor(po.t[:], po.t[:], msk.t[:], ALU.mult), reads=[po.res, msk.res], writes=[po.res])
            dve.op(lambda e: e.tensor_tensor(ln.t[:], ln.t[:], po.t[:], ALU.add), reads=[po.res, ln.res], writes=[ln.res])
            dve.op(lambda e: e.scalar_tensor_tensor(ga, ln.t[:], -1.0, rate.t[:].unsqueeze(1).to_broadcast([128, NT, 8]),
                                                    ALU.mult, ALU.mult),
                   reads=[ln.res, rate.res], writes=[gpt.res])
            if cfg.debug:
                sp.dma(self.gpar.rearrange("(t p) c -> p t c", p=128), gpt.t[:], reads=[gpt.res], writes=[self.dres("gpar")])

    def log1p_acc(self, sc, y, out, shape, tag):
        fw = self.fw
        act, dve = fw.act, fw.dve
        po = sc.sb("l1p_po" + tag, shape)
        msk = sc.sb("l1p_m" + tag, shape)
        Y, O, P, M = y.t[:], out.t[:], po.t[:], msk.t[:]
        act.op(lambda e: e.activation(out=O, in_=Y, func=AF.Ln, bias=1.0, scale=1.0), reads=[y.res], writes=[out.res])
        dve.op(lambda e: e.tensor_scalar(P, Y, 0.2, -0.25, ALU.mult, ALU.add), reads=[y.res], writes=[po.res])
        for cf in (1.0 / 3, -0.5, 1.0):
            dve.op(lambda e: e.tensor_tensor(P, P, Y, ALU.mult), reads=[po.res, y.res], writes=[po.res])
            dve.op(lambda e: e.tensor_scalar(P, P, cf, None, ALU.add), reads=[po.res], writes=[po.res])
        dve.op(lambda e: e.tensor_tensor(P, P, Y, ALU.mult), reads=[po.res, y.res], writes=[po.res])
        dve.op(lambda e: e.tensor_scalar(M, Y, 0.125, None, ALU.is_lt), reads=[y.res], writes=[msk.res])
        dve.op(lambda e: e.tensor_tensor(P, P, O, ALU.subtract), reads=[po.res, out.res], writes=[po.res])
        dve.op(lambda e: e.tensor_tensor(P, P, M, ALU.mult), reads=[po.res, msk.res], writes=[po.res])
        dve.op(lambda e: e.tensor_tensor(O, O, P, ALU.add), reads=[po.res, out.res], writes=[out.res])

    def lru(self, l):
        cfg, fw = self.cfg, self.fw
        pe, act, dve, pool, sp = fw.pe, fw.act, fw.dve, fw.pool, fw.sp
        S, NO, TC = cfg.S, cfg.NO, cfg.TC
        pv, cst = self.pv, self.cst
        with Scope(fw) as sc:
            ey = sc.sb("ey", [128, 8]); spv = sc.sb("spv", [128, 8]); sp8 = sc.sb("sp8", [128, 8]); nsp8 = sc.sb("nsp8", [128, 8])
            act.op(lambda e: e.activation(out=ey.t[:], in_=pv.t[:, l, PV_LLAM:PV_LLAM + 8], func=AF.Exp, scale=-1.0),
                   reads=[pv.res], writes=[ey.res])
            self.log1p_acc(sc, ey, spv, [128, 8], "lru")
            dve.op(lambda e: e.tensor_scalar(sp8.t[:], spv.t[:], 8.0, None, ALU.mult), reads=[spv.res], writes=[sp8.res])
            dve.op(lambda e: e.tensor_scalar(nsp8.t[:], spv.t[:], -8.0, None, ALU.mult), reads=[spv.res], writes=[nsp8.res])
            gw = sc.sb("gw", [128, 2, 2, 128])
            xl = sc.sb("xl", [128, S + 8])
            xc = sc.sb("xc", [128, S])
            bb = sc.sb("bb", [128, S])
            hsum = sc.sb("hsum", [128, NO])
            ly = sc.sb("ly", [128, NO])
            ost = sc.sb("ost", [128, NO], BF16)
            gps = sc.psring("gps", [128, 512], F32, 4)
            tr = sc.sbring("tr", [128, 512], F32, 3)
            ti = sc.sbring("ti", [128, 512], F32, 3)
            tt = sc.sbring("tt", [128, 512], F32, 3)
            tw = sc.sbring("tw", [128, 512], F32, 3)
            a = xl
            segs = [(0, TC, 2), (TC, S, TC + 6)]
            for c in range(4):
                for z0 in (0, TC + 2, S + 6):
                    n = 4 if z0 == TC + 2 else 2
                    dve.op(lambda e: e.memset(xl.t[:, z0:z0 + n], 0.0), writes=[xl.res], par=(z0 > 0))
                for (t0, t1, off) in segs:
                    sp.dma(xl.t[:, off:off + (t1 - t0)], self.lxT[c * 128:(c + 1) * 128, t0:t1],
                           reads=[self.dres("lxT")], writes=[xl.res], par=True)
                for d in range(2):
                    for g in range(2):
                        sp.dma(gw.t[:, d, g, :], self.lru_gw[l, d, g, c, :, :], writes=[gw.res], par=(d + g > 0))
                for (t0, t1, off) in segs:
                    n = t1 - t0
                    for o in range(5):
                        tap = self.PV(l, PV_LCW + c * 5 + o)
                        src = xl.t[:, off + o - 2:off + o - 2 + n]
                        if o == 0:
                            dve.op(lambda e: e.tensor_scalar(xc.t[:, t0:t1], src, tap, self.PV(l, PV_LCB + c), ALU.mult, ALU.add),
                                   reads=[xl.res, pv.res], writes=[xc.res], par=(t0 > 0))
                        else:
                            dve.op(lambda e: e.scalar_tensor_tensor(xc.t[:, t0:t1], src, tap, xc.t[:, t0:t1], ALU.mult, ALU.add),
                                   reads=[xl.res, pv.res, xc.res], writes=[xc.res])
                for d in (1, 0):
                    hi = S if d == 1 else NO
                    first = True
                    for (t0, n) in cfg.blocks(0, hi):
                        sl = slice(t0, t0 + n)
                        pr = gps.next(); pi = gps.next()
                        pe.op(lambda e: e.matmul(pr.t[:, :n], gw.t[:, d, 0, :], xc.t[:, sl], start=True, stop=True),
                              reads=[gw.res, xc.res], writes=[pr.res])
                        pe.op(lambda e: e.matmul(pi.t[:, :n], gw.t[:, d, 1, :], xc.t[:, sl], start=True, stop=True),
                              reads=[gw.res, xc.res], writes=[pi.res])
                        r = tr.next(); ii = ti.next(); th = tt.next(); w_ = tw.next()
                        act.op(lambda e: e.activation(out=r.t[:, :n], in_=pr.t[:, :n], func=AF.Sigmoid,
                                                      bias=self.PV(l, PV_LGB + (d * 2 + 0) * 4 + c)),
                               reads=[pr.res, pv.res], writes=[r.res])
                        act.op(lambda e: e.activation(out=ii.t[:, :n], in_=pi.t[:, :n], func=AF.Sigmoid,
                                                      bias=self.PV(l, PV_LGB + (d * 2 + 1) * 4 + c)),
                               reads=[pi.res, pv.res], writes=[ii.res])
                        aoff = 2 if t0 < TC else 6
                        asl = slice(t0 + aoff, t0 + aoff + n)
                        act.op(lambda e: e.activation(out=a.t[:, asl], in_=r.t[:, :n], func=AF.Exp,
                                                      scale=nsp8.t[:, d * 4 + c:d * 4 + c + 1]),
                               reads=[r.res, nsp8.res, xc.res], writes=[a.res], par=not first)
                        act.op(lambda e: e.activation(out=th.t[:, :n], in_=r.t[:, :n], func=AF.Tanh,
                                                      scale=sp8.t[:, d * 4 + c:d * 4 + c + 1]),
                               reads=[r.res, sp8.res], writes=[th.res])
                        pool.op(lambda e: e.tensor_tensor(w_.t[:, :n], a.t[:, asl], a.t[:, asl], ALU.mult),
                                reads=[a.res], writes=[w_.res])
                        dve.op(lambda e: e.scalar_tensor_tensor(w_.t[:, :n], w_.t[:, :n], 1.0, th.t[:, :n], ALU.add, ALU.mult),
                               reads=[w_.res, th.res], writes=[w_.res])
                        act.op(lambda e: e.activation(out=w_.t[:, :n], in_=w_.t[:, :n], func=AF.Sqrt),
                               reads=[w_.res], writes=[w_.res])
                        pool.op(lambda e: e.tensor_tensor(w_.t[:, :n], w_.t[:, :n], ii.t[:, :n], ALU.mult),
                                reads=[w_.res, ii.res], writes=[w_.res])
                        dve.op(lambda e: e.tensor_tensor(bb.t[:, sl], w_.t[:, :n], xc.t[:, sl], ALU.mult),
                               reads=[w_.res, xc.res], writes=[bb.res], par=not first)
                        first = False
                    A_c = a.t[:, 2:2 + TC]; B_c = bb.t[:, 0:TC]
                    if d == 0:
                        dve.op(lambda e: e.tensor_tensor_scan(B_c, A_c, B_c, 0.0, ALU.mult, ALU.add),
                               reads=[a.res, bb.res], writes=[bb.res])
                        A_l = a.t[:, TC + 6:NO + 6]; B_l = bb.t[:, TC:NO]
                        dve.op(lambda e: e.tensor_tensor_scan(B_l, A_l, B_l, bb.t[:, TC - 1:TC], ALU.mult, ALU.add),
                               reads=[a.res, bb.res], writes=[bb.res])
                        dve.op(lambda e: e.tensor_tensor(hsum.t[:], hsum.t[:], bb.t[:, 0:NO], ALU.add),
                               reads=[hsum.res, bb.res], writes=[hsum.res])
                    else:
                        dve.op(lambda e: e.tensor_tensor_scan(B_c[:, ::-1], A_c[:, ::-1], B_c[:, ::-1], 0.0, ALU.mult, ALU.add),
                               reads=[a.res, bb.res], writes=[bb.res])
                        A_l = a.t[:, TC + 6:S + 6]; B_l = bb.t[:, TC:S]
                        dve.op(lambda e: e.tensor_tensor_scan(B_l[:, ::-1], A_l[:, ::-1], B_l[:, ::-1], bb.t[:, 0:1], ALU.mult, ALU.add),
                               reads=[a.res, bb.res], writes=[bb.res])
                        pool.op(lambda e: e.tensor_copy(hsum.t[:], bb.t[:, 0:NO]), reads=[bb.res], writes=[hsum.res])
                sp.dma(ly.t[:], self.lyT[c * 128:(c + 1) * 128, :], reads=[self.dres("lyT")], writes=[ly.res])
                dve.op(lambda e: e.tensor_tensor(ost.t[:], hsum.t[:], ly.t[:], ALU.mult),
                       reads=[hsum.res, ly.res], writes=[ost.res])
                sp.dma(self.oT[1, c * 128:(c + 1) * 128, :], ost.t[:], reads=[ost.res], writes=[self.dres("oT")], par=True)

    def da(self, l):
        cfg, fw = self.cfg, self.fw
        pe, act, dve, pool, sp = fw.pe, fw.act, fw.dve, fw.pool, fw.sp
        S, NO, TC = cfg.S, cfg.NO, cfg.TC
        pv, cst = self.pv, self.cst
        lam_init = 0.8 - 0.6 * math.exp(-0.3 * (l + self.lbase))
        with Scope(fw) as sc:
            prod = sc.sb("prod", [128, 2]); lame = sc.sb("lame", [128, 2]); neglam = sc.sb("neglam", [128, 1])
            subg = sc.sb("subg", [128, 1])
            lps = sc.ps("lps", [128, 512])
            for i in range(2):
                dve.op(lambda e: e.tensor_tensor(prod.t[:, i:i + 1], self.PV(l, PV_LAM + 2 * i), self.PV(l, PV_LAM + 2 * i + 1), ALU.mult),
                       reads=[pv.res], writes=[prod.res], par=(i > 0))
            pe.op(lambda e: e.matmul(lps.t[:, 0:2], self.ones, prod.t[:], start=True, stop=True),
                  reads=[prod.res, cst.res], writes=[lps.res])
            act.op(lambda e: e.activation(out=lame.t[:], in_=lps.t[:, 0:2], func=AF.Exp), reads=[lps.res], writes=[lame.res])
            dve.op(lambda e: e.tensor_tensor(neglam.t[:], lame.t[:, 1:2], lame.t[:, 0:1], ALU.subtract),
                   reads=[lame.res], writes=[neglam.res])
            dve.op(lambda e: e.tensor_scalar(neglam.t[:], neglam.t[:], -lam_init, None, ALU.add),
                   reads=[neglam.res], writes=[neglam.res])
            dve.op(lambda e: e.tensor_scalar(subg.t[:], self.PV(l, PV_SUBG), 1.0 - lam_init, None, ALU.mult),
                   reads=[pv.res], writes=[subg.res])

            NT = S // 128
            kh = sc.sb("kh", [128, S], BF16)
            vh = sc.sb("vh", [128, NT, 128], BF16)
            qh = sc.sb("qh", [128, NO], BF16)
            stp = sc.psring("stp", [128, 512], F32, 3)
            otp = [sc.ps("ot0", [128, 512]), sc.ps("ot1", [128, 512])]
            smp = sc.ps("smp", [128, 512])
            ptr = sc.sbring("pt", [128, 512], BF16, 4)
            accD = sc.sb("accD", [128, 512]); accP = sc.sb("accP", [128, 512])
            rec = [sc.sb("rec0", [128, 512]), sc.sb("rec1", [128, 512])]
            t0b = sc.sb("t0b", [128, 512]); t1b = sc.sb("t1b", [128, 512]); ob = sc.sb("ob", [128, 512])
            sqb = sc.sb("sqb", [128, 512]); rsb = sc.sb("rsb", [128, 512])
            ostr = sc.sbring("ost", [128, 512], BF16, 2)
            for h in range(4):
                sp.dma(kh.t[:], self.kT[h * 128:(h + 1) * 128, :], reads=[self.dres("kT")], writes=[kh.res])
                for t4 in range(0, NT, 16):
                    t5 = min(NT, t4 + 16)
                    sp.dma(vh.t[:, t4:t5, :],
                           self.v_tok[t4 * 128:t5 * 128, h * 128:(h + 1) * 128].rearrange("(t p) d -> p t d", p=128),
                           reads=[self.dres("v_tok")], writes=[vh.res], par=(t4 > 0))
                sp.dma(qh.t[:], self.qT[h * 128:(h + 1) * 128, :], reads=[self.dres("qT")], writes=[qh.res])
                for (q0, n) in cfg.blocks(0, NO):
                    ktiles = list(range(0, TC // 128)) if q0 < TC else list(range(NT))
                    for c in range(2):
                        ps = slice(c * 64, (c + 1) * 64)
                        usedD = usedP = False
                        for ki, kt in enumerate(ktiles):
                            st = stp.next()
                            pe.op(lambda e: e.matmul(st.t[:, :n], kh.t[ps, kt * 128:(kt + 1) * 128], qh.t[ps, q0:q0 + n],
                                                     start=True, stop=True),
                                  reads=[kh.res, qh.res], writes=[st.res])
                            pt = ptr.next()
                            act.op(lambda e: e.activation(out=pt.t[:, :n], in_=st.t[:, :n], func=AF.Exp, scale=0.125),
                                   reads=[st.res], writes=[pt.res])
                            pe.op(lambda e: e.matmul(otp[c].t[:, :n], vh.t[:, kt, :], pt.t[:, :n],
                                                     start=(ki == 0), stop=(ki == len(ktiles) - 1)),
                                  reads=[vh.res, pt.res], writes=[otp[c].res])
                            if ki % 2 == 0:
                                if not usedD:
                                    dve.op(lambda e: e.tensor_copy(accD.t[:, :n], pt.t[:, :n]), reads=[pt.res], writes=[accD.res])
                                else:
                                    dve.op(lambda e: e.tensor_tensor(accD.t[:, :n], accD.t[:, :n], pt.t[:, :n], ALU.add),
                                           reads=[pt.res, accD.res], writes=[accD.res])
                                usedD = True
                            else:
                                if not usedP:
                                    pool.op(lambda e: e.tensor_copy(accP.t[:, :n], pt.t[:, :n]), reads=[pt.res], writes=[accP.res])
                                else:
                                    pool.op(lambda e: e.tensor_tensor(accP.t[:, :n], accP.t[:, :n], pt.t[:, :n], ALU.add),
                                            reads=[pt.res, accP.res], writes=[accP.res])
                                usedP = True
                        pe.op(lambda e: e.matmul(smp.t[:, :n], self.ones, accD.t[:, :n], start=True, stop=not usedP),
                              reads=[accD.res, cst.res], writes=[smp.res])
                        if usedP:
                            pe.op(lambda e: e.matmul(smp.t[:, :n], self.ones, accP.t[:, :n], start=False, stop=True),
                                  reads=[accP.res, cst.res], writes=[smp.res])
                        dve.op(lambda e: e.reciprocal(rec[c].t[:, :n], smp.t[:, :n]), reads=[smp.res], writes=[rec[c].res])
                    dve.op(lambda e: e.tensor_tensor(t0b.t[:, :n], otp[0].t[:, :n], rec[0].t[:, :n], ALU.mult),
                           reads=[otp[0].res, rec[0].res], writes=[t0b.res])
                    dve.op(lambda e: e.tensor_tensor(t1b.t[:, :n], otp[1].t[:, :n], rec[1].t[:, :n], ALU.mult),
                           reads=[otp[1].res, rec[1].res], writes=[t1b.res])
                    dve.op(lambda e: e.scalar_tensor_tensor(ob.t[:, :n], t1b.t[:, :n], neglam.t[:, 0:1], t0b.t[:, :n], ALU.mult, ALU.add),
                           reads=[t0b.res, t1b.res, neglam.res], writes=[ob.res])
                    act.op(lambda e: e.activation(out=sqb.t[:, :n], in_=ob.t[:, :n], func=AF.Square), reads=[ob.res], writes=[sqb.res])
                    pe.op(lambda e: e.matmul(smp.t[:, :n], self.ones, sqb.t[:, :n], start=True, stop=True),
                          reads=[sqb.res, cst.res], writes=[smp.res])
                    act.op(lambda e: e.activation(out=rsb.t[:, :n], in_=smp.t[:, :n], func=AF.Sqrt, scale=1.0 / 128,
                                                  bias=self.epsb.t[:, 0:1]),
                           reads=[smp.res, self.epsb.res], writes=[rsb.res])
                    dve.op(lambda e: e.reciprocal(rsb.t[:, :n], rsb.t[:, :n]), reads=[rsb.res], writes=[rsb.res])
                    ost = ostr.next()
                    dve.op(lambda e: e.scalar_tensor_tensor(ost.t[:, :n], ob.t[:, :n], subg.t[:, 0:1], rsb.t[:, :n], ALU.mult, ALU.mult),
                           reads=[ob.res, subg.res, rsb.res], writes=[ost.res])
                    sp.dma(self.oT[0, h * 128:(h + 1) * 128, q0:q0 + n], ost.t[:, :n], reads=[ost.res],
                           writes=[self.dres("oT")], par=True)

    def gdn(self, l):
        cfg, fw = self.cfg, self.fw
        pe, act, dve, pool, sp = fw.pe, fw.act, fw.dve, fw.pool, fw.sp
        S, NO, TC = cfg.S, cfg.NO, cfg.TC
        pv, cst, gpt = self.pv, self.cst, self.gpt
        NT, NCC, NOC = S // 128, TC // 128, NO // 128
        ident, ones = self.ident, self.ones

        def run_interleaved(gens):
            gens = list(gens)
            while gens:
                nxt = []
                for g in gens:
                    try:
                        next(g)
                        nxt.append(g)
                    except StopIteration:
                        pass
                gens = nxt

        with Scope(fw) as sc:
            qTh = sc.sb("qTh", [128, S], BF16)
            kTh = sc.sb("kTh", [128, S], BF16)
            ktok = sc.sb("ktok", [128, NT, 128], BF16)
            vtok = sc.sb("vtok", [128, NT, 128], BF16)
            obuf = [sc.sb("ofw", [128, NOC, 128]), sc.sb("orv", [128, NOC, 128])]
            gz = sc.sb("gz", [128, NO])
            class BankPool:
                def __init__(self, bufs):
                    self.free = list(bufs)

                def take(self):
                    return self.free.pop(0)

                def give(self, b):
                    self.free.append(b)
            fb = BankPool(sc.psring("gfb", [128, 512], F32, 6).bufs)
            bbk = BankPool(sc.psring("gbb", [128, 1024], BF16, 2).bufs)

            def Q(bank, i):
                return bank.t[:, i * 128:(i + 1) * 128]
            rw = sc.sbring("rw", [128, 516], F32, 2)
            cvb = sc.sbring("cvb", [128, 512], F32, 2)
            svb = sc.sbring("svb", [128, 512], F32, 2)
            sqb = sc.sbring("gsq", [128, 512], F32, 2)
            rnb = sc.sbring("rnb", [128, 512], F32, 2)
            vtb = sc.sbring("vtb", [128, 512], BF16, 2)

            class PS:
                pass
            presets = {}
            for d in range(2):
                for par in range(2):
                    P = PS()
                    nm = f"p{d}{par}"
                    P.cols = sc.sb(nm + "cols", [128, 8]); P.g2 = sc.sb(nm + "g2", [128, 2])
                    P.Gb = sc.sb(nm + "Gb", [128, 128]); P.DT = sc.sb(nm + "DT", [128, 128]); P.DTs = sc.sb(nm + "DTs", [128, 128])
                    P.kb = sc.sb(nm + "kb", [128, 128], BF16); P.kbT = sc.sb(nm + "kbT", [128, 128], BF16)
                    P.Pm = [sc.sb(nm + "Pm0", [128, 128]), sc.sb(nm + "Pm1", [128, 128])]
                    P.Q = [sc.sb(nm + "Q0", [128, 128]), sc.sb(nm + "Q1", [128, 128])]
                    P.R = [sc.sb(nm + "R0", [128, 128]), sc.sb(nm + "R1", [128, 128])]
                    P.Qi = sc.sb(nm + "Qi", [128, 128])
                    P.Tb = sc.sb(nm + "Tb", [128, 128], BF16); P.negwT = sc.sb(nm + "nw", [128, 128], BF16)
                    P.attnT = sc.sb(nm + "at", [128, 128], BF16); P.kdec = sc.sb(nm + "kd", [128, 128], BF16)
                    P.vb = sc.sb(nm + "vb", [128, 128], BF16); P.kbe = sc.sb(nm + "kbe", [128, 128], BF16)
                    presets[(d, par)] = P
            states = []
            for d in range(2):
                st = PS()
                st.Sf = sc.sb(f"Sf{d}", [128, 128]); st.Sb = sc.sb(f"Sb{d}", [128, 128], BF16)
                st.vnew = sc.sb(f"vn{d}", [128, 128], BF16); st.o1s = sc.sb(f"o1s{d}", [128, 128])
                states.append(st)
            osum = sc.sbring("osum", [128, 128], F32, 2); ojunk = sc.sbring("ojunk", [128, 128], F32, 2)
            ocol = sc.sbring("ocol", [128, 2], F32, 2); onb = sc.sbring("onb", [128, 128], BF16, 2)
            ostg = sc.sbring("gost", [128, 128], BF16, 3)

            tri = [self.triu, self.tril]
            negm = [self.negu, self.negl]
            strict = [self.stru, self.strl]

            import os
            gstop = int(os.environ.get("GSTOP", "99"))
            for h in range(4):
                if gstop < 99 and h > 0:
                    break
                for which, cc in (("q", h), ("k", 4 + h), ("v", 8 + h)):
                    if (which == "k" and gstop < 2) or (which == "v" and gstop < 3):
                        continue
                    for (t0, n) in cfg.blocks(0, S):
                        seg0, seg1 = (0, TC) if t0 < TC else (TC, S)
                        lo, hi = max(seg0, t0 - 2), min(seg1, t0 + n + 2)
                        r = rw.next()
                        dve.op(lambda e: e.memset(r.t[:, 0:2], 0.0), writes=[r.res])
                        dve.op(lambda e: e.memset(r.t[:, n + 2:n + 4], 0.0), writes=[r.res])
                        sp.dma(r.t[:, lo - (t0 - 2):hi - (t0 - 2)], self.gT[cc * 128:(cc + 1) * 128, lo:hi],
                               reads=[self.dres("gT")], writes=[r.res])
                        cv = cvb.next()
                        ceng = dve if (t0 // 512) % 2 == 0 else pool
                        for o in range(5):
                            tap = self.PV(l, PV_GCW + cc * 5 + o)
                            src = r.t[:, o:o + n]
                            if o == 0:
                                ceng.op(lambda e: e.tensor_scalar(cv.t[:, :n], src, tap, None, ALU.mult),
                                        reads=[r.res, pv.res], writes=[cv.res])
                            else:
                                dve.op(lambda e: e.scalar_tensor_tensor(cv.t[:, :n], src, tap, cv.t[:, :n], ALU.mult, ALU.add),
                                       reads=[r.res, pv.res, cv.res], writes=[cv.res])
                        sv = svb.next()
                        if which == "v":
                            vt = vtb.next()
                            act.op(lambda e: e.activation(out=vt.t[:, :n], in_=cv.t[:, :n], func=AF.Silu), reads=[cv.res], writes=[vt.res])
                            src_b = vt
                        else:
                            act.op(lambda e: e.activation(out=sv.t[:, :n], in_=cv.t[:, :n], func=AF.Silu), reads=[cv.res], writes=[sv.res])
                            sq = sqb.next()
                            pool.op(lambda e: e.tensor_tensor(sq.t[:, :n], sv.t[:, :n], sv.t[:, :n], ALU.mult), reads=[sv.res], writes=[sq.res])
                            ssp = fb.take()
                            pe.op(lambda e: e.matmul(ssp.t[:, :n], ones, sq.t[:, :n], start=True, stop=True),
                                  reads=[sq.res, cst.res], writes=[ssp.res])
                            rn = rnb.next()
                            act.op(lambda e: e.activation(out=rn.t[:, :n], in_=ssp.t[:, :n], func=AF.Ln, bias=self.epsb.t[:, 0:1]),
                                   reads=[ssp.res, self.epsb.res], writes=[rn.res])
                            fb.give(ssp)
                            act.op(lambda e: e.activation(out=rn.t[:, :n], in_=rn.t[:, :n], func=AF.Exp, scale=-0.5,
                                                          bias=(self.lnq.t[:, 0:1] if which == "q" else self.zerob.t[:, 0:1])),
                                   reads=[rn.res, self.lnq.res, self.zerob.res], writes=[rn.res])
                            dst = qTh if which == "q" else kTh
                            dve.op(lambda e: e.tensor_tensor(dst.t[:, t0:t0 + n], sv.t[:, :n], rn.t[:, :n], ALU.mult),
                                   reads=[sv.res, rn.res], writes=[dst.res], par=(t0 > 0))
                            src_b = None
                        if which in ("k", "v"):
                            nsub = n // 128
                            tl0 = t0 // 128
                            tp = bbk.take()
                            srcb = kTh if which == "k" else src_b
                            for sub in range(nsub):
                                so = (t0 + sub * 128) if which == "k" else sub * 128
                                pe.op(lambda e: e.transpose(Q(tp, sub), srcb.t[:, so:so + 128], self.identb.t[:]),
                                      reads=[srcb.res, self.identb.res], writes=[tp.res])
                            dstt = ktok if which == "k" else vtok
                            act.op(lambda e: e.copy(dstt.t[:, tl0:tl0 + nsub, :],
                                                    tp.t[:, 0:nsub * 128].rearrange("p (c n) -> p c n", c=nsub)),
                                   reads=[tp.res], writes=[dstt.res], par=(tl0 > 0))
                            bbk.give(tp)
                sp.dma(gz.t[:], self.gzT[h * 128:(h + 1) * 128, :], reads=[self.dres("gzT")], writes=[gz.res])

                def pre(d, n, P):
                    beta = gpt.t[:, n, d * 4 + h:d * 4 + h + 1]
                    g = gpt.t[:, n, 8 + d * 4 + h:8 + d * 4 + h + 1]
                    csl = slice(n * 128, (n + 1) * 128)
                    need_o = n < NOC
                    C = P.cols
                    dve.op(lambda e: e.tensor_scalar(P.Gb.t[:], ones, g, None, ALU.mult), reads=[cst.res, gpt.res], writes=[P.Gb.res])
                    act.op(lambda e: e.activation(out=P.kb.t[:], in_=ktok.t[:, n, :], func=AF.Identity, scale=beta),
                           reads=[ktok.res, gpt.res], writes=[P.kb.res])
                    dve.op(lambda e: e.tensor_copy(P.g2.t[:, 0:1], g), reads=[gpt.res], writes=[P.g2.res])
                    dve.op(lambda e: e.tensor_copy(P.g2.t[:, 1:2], g), reads=[gpt.res], writes=[P.g2.res])
                    yield
                    while not fb.free or not bbk.free:
                        yield
                    bA = fb.take()
                    pe.op(lambda e: e.matmul(bA.t[:, 0:2], tri[d], P.g2.t[:], start=True, stop=True), reads=[cst.res, P.g2.res], writes=[bA.res])
                    pe.op(lambda e: e.matmul(bA.t[:, 2:4], ones, P.g2.t[:], start=True, stop=True), reads=[cst.res, P.g2.res], writes=[bA.res])
                    pe.op(lambda e: e.matmul(Q(bA, 1), P.Gb.t[:], tri[d], start=True, stop=False), reads=[P.Gb.res, cst.res], writes=[bA.res])
                    pe.op(lambda e: e.matmul(Q(bA, 1), ident, negm[d], start=False, stop=True), reads=[cst.res], writes=[bA.res])
                    bT = bbk.take()
                    pe.op(lambda e: e.transpose(Q(bT, 0), P.kb.t[:], self.identb.t[:]), reads=[P.kb.res, self.identb.res], writes=[bT.res])
                    yield
                    dve.op(lambda e: e.tensor_copy(C.t[:, 0:2], bA.t[:, 1:3]), reads=[bA.res], writes=[C.res])
                    dve.op(lambda e: e.tensor_scalar(C.t[:, 2:3], C.t[:, 0:1], -1.0, None, ALU.mult), reads=[C.res], writes=[C.res])
                    dve.op(lambda e: e.tensor_copy(P.kbT.t[:], Q(bT, 0)), reads=[bT.res], writes=[P.kbT.res])
                    bbk.give(bT)
                    yield
                    act.op(lambda e: e.activation(out=C.t[:, 3:4], in_=C.t[:, 0:1], func=AF.Exp), reads=[C.res], writes=[C.res])
                    act.op(lambda e: e.activation(out=C.t[:, 4:5], in_=C.t[:, 0:1], func=AF.Exp, scale=-1.0, bias=C.t[:, 1:2]),
                           reads=[C.res], writes=[C.res])
                    act.op(lambda e: e.activation(out=C.t[:, 5:6], in_=C.t[:, 1:2], func=AF.Exp), reads=[C.res], writes=[C.res])
                    act.op(lambda e: e.activation(out=P.DT.t[:], in_=Q(bA, 1), func=AF.Exp, bias=C.t[:, 2:3]),
                           reads=[bA.res, C.res], writes=[P.DT.res])
                    fb.give(bA)
                    while not fb.free:
                        yield
                    bK = fb.take()
                    pe.op(lambda e: e.matmul(Q(bK, 0), kTh.t[:, csl], P.kbT.t[:], start=True, stop=True),
                          reads=[kTh.res, P.kbT.res], writes=[bK.res])
                    if need_o:
                        pe.op(lambda e: e.matmul(Q(bK, 1), kTh.t[:, csl], qTh.t[:, csl], start=True, stop=True),
                              reads=[kTh.res, qTh.res], writes=[bK.res])
                    yield
                    pool.op(lambda e: e.tensor_tensor(P.DTs.t[:], P.DT.t[:], strict[d], ALU.mult), reads=[P.DT.res, cst.res], writes=[P.DTs.res])
                    yield
                    Pm, Qm, R = P.Pm, P.Q, P.R
                    dve.op(lambda e: e.tensor_tensor(Pm[0].t[:], Q(bK, 0), P.DTs.t[:], ALU.mult), reads=[bK.res, P.DTs.res], writes=[Pm[0].res])
                    if need_o:
                        dve.op(lambda e: e.tensor_tensor(P.attnT.t[:], Q(bK, 1), P.DT.t[:], ALU.mult),
                               reads=[bK.res, P.DT.res], writes=[P.attnT.res])
                    fb.give(bK)
                    yield
                    while not fb.free:
                        yield
                    bM = fb.take()
                    pe.op(lambda e: e.transpose(Q(bM, 0), Pm[0].t[:], ident), reads=[Pm[0].res, cst.res], writes=[bM.res])
                    pool.op(lambda e: e.tensor_tensor(R[0].t[:], ident, Pm[0].t[:], ALU.subtract), reads=[Pm[0].res, cst.res], writes=[R[0].res])
                    yield
                    act.op(lambda e: e.copy(Qm[0].t[:], Q(bM, 0)), reads=[bM.res], writes=[Qm[0].res])
                    fb.give(bM)
                    yield
                    cur = 0
                    for k in range(1, 7):
                        nx = 1 - cur
                        while not fb.free:
                            yield
                        bL = fb.take()
                        if k < 6:
                            pe.op(lambda e: e.matmul(Q(bL, 0), Qm[cur].t[:], Pm[cur].t[:], start=True, stop=True),
                                  reads=[Qm[cur].res, Pm[cur].res], writes=[bL.res])
                        pe.op(lambda e: e.matmul(Q(bL, 1), Pm[cur].t[:], Qm[cur].t[:], start=True, stop=True),
                              reads=[Qm[cur].res, Pm[cur].res], writes=[bL.res])
                        yield
                        act.op(lambda e: e.copy(Qm[nx].t[:], Q(bL, 1)), reads=[bL.res], writes=[Qm[nx].res])
                        if k < 6:
                            act.op(lambda e: e.copy(Pm[nx].t[:], Q(bL, 0)), reads=[bL.res], writes=[Pm[nx].res])
                        pool.op(lambda e: e.tensor_tensor(P.Qi.t[:], Qm[nx].t[:], ident, ALU.add), reads=[Qm[nx].res, cst.res], writes=[P.Qi.res])
                        fb.give(bL)
                        yield
                        while not fb.free:
                            yield
                        bR = fb.take()
                        pe.op(lambda e: e.matmul(Q(bR, 0), P.Qi.t[:], R[cur].t[:], start=True, stop=True),
                              reads=[P.Qi.res, R[cur].res], writes=[bR.res])
                        yield
                        dve.op(lambda e: e.tensor_copy(R[nx].t[:], Q(bR, 0)), reads=[bR.res], writes=[R[nx].res])
                        fb.give(bR)
                        cur = nx
                        yield
                    T = R[cur]
                    dve.op(lambda e: e.tensor_copy(P.Tb.t[:], T.t[:]), reads=[T.res], writes=[P.Tb.res])
                    act.op(lambda e: e.activation(out=P.kbe.t[:], in_=P.kb.t[:], func=AF.Identity, scale=C.t[:, 3:4]),
                           reads=[P.kb.res, C.res], writes=[P.kbe.res])
                    act.op(lambda e: e.activation(out=P.kdec.t[:], in_=ktok.t[:, n, :], func=AF.Identity, scale=C.t[:, 4:5]),
                           reads=[ktok.res, C.res], writes=[P.kdec.res])
                    act.op(lambda e: e.activation(out=P.vb.t[:], in_=vtok.t[:, n, :], func=AF.Identity, scale=beta),
                           reads=[vtok.res, gpt.res], writes=[P.vb.res])
                    yield
                    while not fb.free:
                        yield
                    bW = fb.take()
                    pe.op(lambda e: e.matmul(Q(bW, 0), P.kbe.t[:], P.Tb.t[:], start=True, stop=True),
                          reads=[P.kbe.res, P.Tb.res], writes=[bW.res])
                    yield
                    act.op(lambda e: e.activation(out=P.negwT.t[:], in_=Q(bW, 0), func=AF.Identity, scale=-1.0),
                           reads=[bW.res], writes=[P.negwT.res])
                    fb.give(bW)

                def chain(d, n, P, st):
                    csl = slice(n * 128, (n + 1) * 128)
                    need_o = n < NOC
                    C = P.cols
                    while not fb.free:
                        yield
                    bV = fb.take()
                    pe.op(lambda e: e.matmul(Q(bV, 0), P.Tb.t[:], P.vb.t[:], start=True, stop=False),
                          reads=[P.Tb.res, P.vb.res], writes=[bV.res])
                    pe.op(lambda e: e.matmul(Q(bV, 0), P.negwT.t[:], st.Sb.t[:], start=False, stop=True),
                          reads=[P.negwT.res, st.Sb.res], writes=[bV.res])
                    if need_o:
                        pe.op(lambda e: e.matmul(Q(bV, 1), qTh.t[:, csl], st.Sb.t[:], start=True, stop=True),
                              reads=[qTh.res, st.Sb.res], writes=[bV.res])
                    yield
                    act.op(lambda e: e.copy(st.vnew.t[:], Q(bV, 0)), reads=[bV.res], writes=[st.vnew.res])
                    if need_o:
                        act.op(lambda e: e.activation(out=st.o1s.t[:], in_=Q(bV, 1), func=AF.Identity, scale=C.t[:, 3:4]),
                               reads=[bV.res, C.res], writes=[st.o1s.res])
                    fb.give(bV)
                    yield
                    while not fb.free:
                        yield
                    bK2 = fb.take()
                    pe.op(lambda e: e.matmul(Q(bK2, 0), P.kdec.t[:], st.vnew.t[:], start=True, stop=True),
                          reads=[P.kdec.res, st.vnew.res], writes=[bK2.res])
                    if need_o:
                        pe.op(lambda e: e.matmul(Q(bK2, 1), P.attnT.t[:], st.vnew.t[:], start=True, stop=True),
                              reads=[P.attnT.res, st.vnew.res], writes=[bK2.res])
                    yield
                    dve.op(lambda e: e.scalar_tensor_tensor(st.Sf.t[:], st.Sf.t[:], C.t[:, 5:6], Q(bK2, 0), ALU.mult, ALU.add),
                           reads=[st.Sf.res, C.res, bK2.res], writes=[st.Sf.res])
                    if need_o:
                        dve.op(lambda e: e.tensor_tensor(obuf[d].t[:, n, :], st.o1s.t[:], Q(bK2, 1), ALU.add),
                               reads=[st.o1s.res, bK2.res], writes=[obuf[d].res], par=True)
                    fb.give(bK2)
                    yield
                    act.op(lambda e: e.copy(st.Sb.t[:], st.Sf.t[:]), reads=[st.Sf.res], writes=[st.Sb.res])

                if gstop < 4:
                    continue
                seqs = [list(range(NOC)), list(range(NCC - 1, -1, -1)) + list(range(NT - 1, NCC - 1, -1))]
                for d in range(2):
                    dve.op(lambda e: e.memset(states[d].Sf.t[:], 0.0), writes=[states[d].Sf.res])
                    dve.op(lambda e: e.memset(states[d].Sb.t[:], 0.0), writes=[states[d].Sb.res])
                    dve.op(lambda e: e.memset(obuf[d].t[:, 0, 0:1], 0.0), writes=[obuf[d].res])
                pstop = int(os.environ.get("PSTOP", "9999"))

                def limited(gen, k):
                    for i, _ in enumerate(gen):
                        if i >= k:
                            return
                        yield
                run_interleaved([limited(pre(d, seqs[d][0], presets[(d, 0)]), pstop) for d in range(2)])
                if gstop < 5:
                    continue
                for i in range(len(seqs[1])):
                    gens = []
                    for d in range(2):
                        if i < len(seqs[d]):
                            gens.append(chain(d, seqs[d][i], presets[(d, i % 2)], states[d]))
                    for d in range(2):
                        if i + 1 < len(seqs[d]):
                            gens.append(pre(d, seqs[d][i + 1], presets[(d, (i + 1) % 2)]))
                    run_interleaved(gens)

                if gstop < 6:
                    continue
                for n in range(NOC):
                    os_ = osum.next(); oj = ojunk.next(); oc = ocol.next(); on = onb.next()
                    dve.op(lambda e: e.tensor_tensor(os_.t[:], obuf[0].t[:, n, :], obuf[1].t[:, n, :], ALU.add),
                           reads=[obuf[0].res, obuf[1].res], writes=[os_.res])
                    act.op(lambda e: e.activation(out=oj.t[:], in_=os_.t[:], func=AF.Square, accum_out=oc.t[:, 0:1]),
                           reads=[os_.res], writes=[oj.res, oc.res])
                    act.op(lambda e: e.activation(out=oc.t[:, 1:2], in_=oc.t[:, 0:1], func=AF.Sqrt, scale=1.0 / 128, bias=self.epsb.t[:, 0:1]),
                           reads=[oc.res, self.epsb.res], writes=[oc.res])
                    dve.op(lambda e: e.reciprocal(oc.t[:, 1:2], oc.t[:, 1:2]), reads=[oc.res], writes=[oc.res])
                    act.op(lambda e: e.activation(out=on.t[:], in_=os_.t[:], func=AF.Identity, scale=oc.t[:, 1:2]),
                           reads=[os_.res, oc.res], writes=[on.res])
                    tp = bbk.take()
                    pe.op(lambda e: e.transpose(Q(tp, 0), on.t[:], self.identb.t[:]), reads=[on.res, self.identb.res], writes=[tp.res])
                    og = ostg.next()
                    dve.op(lambda e: e.scalar_tensor_tensor(og.t[:], Q(tp, 0), self.PV(l, PV_GNG), gz.t[:, n * 128:(n + 1) * 128],
                                                            ALU.mult, ALU.mult),
                           reads=[tp.res, pv.res, gz.res], writes=[og.res])
                    bbk.give(tp)
                    sp.dma(self.oT[2, h * 128:(h + 1) * 128, n * 128:(n + 1) * 128], og.t[:], reads=[og.res],
                           writes=[self.dres("oT")], par=True)

    def p3a(self, l, last=False):
        cfg, fw = self.cfg, self.fw
        pe, act, dve, pool, sp = fw.pe, fw.act, fw.dve, fw.pool, fw.sp
        S, NO, TC = cfg.S, cfg.NO, cfg.TC
        with Scope(fw) as sc:
            wbr = sc.sb("wbr", [128, 3, 4, D], BF16)
            for i in range(3):
                sp.dma(wbr.t[:, i, :, :], self.wbr_bf[l, i].rearrange("(kc p) n -> p kc n", p=128),
                       reads=[self.dres(f"w3a{l}")], writes=[wbr.res], par=(i > 0))
            wo = sc.sb("wo", [128, 8, D], BF16)
            sp.dma(wo.t[:], self.wout_bf[l].rearrange("(kc p) n -> p kc n", p=128), reads=[self.dres(f"w3a{l}")], writes=[wo.res])
            obr = sc.sbring("obr", [128, 3, 4, 512], BF16, 2)
            gtr = sc.sbring("gtr", [128, 24, 512], BF16, 1)
            xbr = sc.sbring("x3a", [128, 8, 512], F32, 2)
            mg = sc.sbring("mg", [128, 8, 512], BF16, 1)
            bps = sc.psring("bps", [128, 512], F32, 5)
            yps = sc.psring("yps", [128, 512], F32, 2)
            ma = sc.sbring("ma", [128, 512], F32, 3)
            mb = sc.sbring("mb", [128, 512], F32, 3)
            for (t0, n) in cfg.blocks(TC if last else 0, NO):
                w = 1 if t0 < TC else 0
                ob = obr.next(); gt = gtr.next(); xb = xbr.next(); mgd = mg.next()
                for i in range(3):
                    sp.dma(ob.t[:, i, :, :n], self.oT[i, :, t0:t0 + n].rearrange("(kc p) n -> p kc n", p=128),
                           reads=[self.dres("oT")], writes=[ob.res], par=(i > 0))
                sp.dma(gt.t[:, :, :n], self.gateT[:, t0:t0 + n].rearrange("(c p) n -> p c n", p=128),
                       reads=[self.dres("gateT")], writes=[gt.res])
                sp.dma(xb.t[:, :, :n], self.xT[:, t0:t0 + n].rearrange("(c p) n -> p c n", p=128),
                       reads=[self.dres("xT")], writes=[xb.res])
                for c in range(8):
                    B = []
                    for i in range(3):
                        p = bps.next()
                        for kc in range(4):
                            pe.op(lambda e: e.matmul(p.t[:, :n], wbr.t[:, i, kc, c * 128:(c + 1) * 128], ob.t[:, i, kc, :n],
                                                     start=(kc == 0), stop=(kc == 3)),
                                  reads=[wbr.res, ob.res], writes=[p.res])
                        B.append(p)
                    m0 = ma.next(); m1 = mb.next(); m2 = mb.next()
                    dve.op(lambda e: e.tensor_tensor(m0.t[:, :n], B[0].t[:, :n], gt.t[:, 0 * 8 + c, :n], ALU.mult),
                           reads=[B[0].res, gt.res], writes=[m0.res])
                    dve.op(lambda e: e.tensor_tensor(m1.t[:, :n], B[1].t[:, :n], gt.t[:, 1 * 8 + c, :n], ALU.mult),
                           reads=[B[1].res, gt.res], writes=[m1.res])
                    dve.op(lambda e: e.tensor_tensor(m2.t[:, :n], B[2].t[:, :n], gt.t[:, 2 * 8 + c, :n], ALU.mult),
                           reads=[B[2].res, gt.res], writes=[m2.res])
                    pool.op(lambda e: e.tensor_tensor(m0.t[:, :n], m0.t[:, :n], m1.t[:, :n], ALU.add),
                            reads=[m0.res, m1.res], writes=[m0.res])
                    pool.op(lambda e: e.tensor_tensor(mgd.t[:, c, :n], m0.t[:, :n], m2.t[:, :n], ALU.add),
                            reads=[m0.res, m2.res], writes=[mgd.res], par=(c > 0))
                for c in range(8):
                    y = yps.next()
                    for k in range(8):
                        pe.op(lambda e: e.matmul(y.t[:, :n], wo.t[:, k, c * 128:(c + 1) * 128], mgd.t[:, k, :n],
                                                 start=(k == 0), stop=(k == 7)),
                              reads=[wo.res, mgd.res], writes=[y.res])
                    dve.op(lambda e: e.scalar_tensor_tensor(xb.t[:, c, :n], y.t[:, :n], self.MV(l, 2, c, w), xb.t[:, c, :n],
                                                            ALU.mult, ALU.add),
                           reads=[y.res, xb.res, self.modv.res], writes=[xb.res])
                sp.dma(self.xT[:, t0:t0 + n].rearrange("(c p) n -> p c n", p=128), xb.t[:, :, :n],
                       reads=[xb.res], writes=[self.dres("xT")], par=True)

    def p3b(self, l, last=False, emit=True):
        cfg, fw = self.cfg, self.fw
        pe, act, dve, pool, sp = fw.pe, fw.act, fw.dve, fw.pool, fw.sp
        S, NO, TC = cfg.S, cfg.NO, cfg.TC
        TB = 256
        with Scope(fw) as sc:
            w1s = sc.sb("w1s", [128, 8, DFF], BF16)
            w2s = sc.sb("w2s", [128, 32, D], BF16)
            for k in range(8):
                sp.dma(w1s.t[:, k, :], self.w1_bf[l, k * 128:(k + 1) * 128, :], reads=[self.dres(f"w3b{l}")],
                       writes=[w1s.res], par=(k > 0))
            for k4 in range(0, 32, 8):
                sp.dma(w2s.t[:, k4:k4 + 8, :], self.w2_bf[l, k4 * 128:(k4 + 8) * 128, :].rearrange("(kc p) n -> p kc n", p=128),
                       reads=[self.dres(f"w3b{l}")], writes=[w2s.res], par=(k4 > 0))
            xbr = sc.sbring("x3b", [128, 8, TB], F32, 1)
            hTr = sc.sbring("h2T", [128, 8, TB], BF16, 1)
            aT = sc.sb("aT", [128, 32, TB], BF16)
            tools = (sc.sbring("sq3", [128, TB], F32, 2), sc.sbring("nt3", [128, TB], F32, 2),
                     sc.ps("ss3", [128, 512]), sc.sb("rstd3", [128, TB]))
            ups = sc.psring("ups", [128, 512], F32, 2)
            mps = sc.psring("mps", [128, 512], F32, 2)
            tps = sc.psring("tps3", [128, 512], F32, 2)
            rl = sc.sbring("rl", [128, TB], F32, 3)
            otk = sc.sbring("otk", [128, D], F32, 1)
            for (t0, n) in cfg.blocks(TC if last else 0, NO, tb=TB):
                is_ctx = t0 < TC
                w = 1 if is_ctx else 0
                xb = xbr.next(); hT = hTr.next()
                sp.dma(xb.t[:, :, :n], self.xT[:, t0:t0 + n].rearrange("(c p) n -> p c n", p=128),
                       reads=[self.dres("xT")], writes=[xb.res])
                self.norm_mod(sc, l, xb, n, is_ctx, hT, self.A2, 3, tools)
                for f in range(32):
                    u = ups.next()
                    for k in range(8):
                        pe.op(lambda e: e.matmul(u.t[:, :n], w1s.t[:, k, f * 128:(f + 1) * 128], hT.t[:, k, :n],
                                                 start=(k == 0), stop=(k == 7)),
                              reads=[w1s.res, hT.res], writes=[u.res])
                    r = rl.next()
                    act.op(lambda e: e.activation(out=r.t[:, :n], in_=u.t[:, :n], func=AF.Relu), reads=[u.res], writes=[r.res])
                    pool.op(lambda e: e.tensor_tensor(aT.t[:, f, :n], r.t[:, :n], r.t[:, :n], ALU.mult),
                            reads=[r.res], writes=[aT.res], par=(f > 0))
                for c in range(8):
                    m = mps.next()
                    for f in range(32):
                        pe.op(lambda e: e.matmul(m.t[:, :n], w2s.t[:, f, c * 128:(c + 1) * 128], aT.t[:, f, :n],
                                                 start=(f == 0), stop=(f == 31)),
                              reads=[w2s.res, aT.res], writes=[m.res])
                    dve.op(lambda e: e.scalar_tensor_tensor(xb.t[:, c, :n], m.t[:, :n], self.MV(l, 5, c, w), xb.t[:, c, :n],
                                                            ALU.mult, ALU.add),
                           reads=[m.res, xb.res, self.modv.res], writes=[xb.res])
                if not last:
                    sp.dma(self.xT[:, t0:t0 + n].rearrange("(c p) n -> p c n", p=128), xb.t[:, :, :n],
                           reads=[xb.res], writes=[self.dres("xT")], par=True)
                if emit:
                    for sub in range(n // 128):
                        ot = otk.next()
                        for half in range(2):
                            tp = tps.next()
                            for j in range(4):
                                c = half * 4 + j
                                pe.op(lambda e: e.transpose(tp.t[:, j * 128:(j + 1) * 128], xb.t[:, c, sub * 128:(sub + 1) * 128], self.ident),
                                      reads=[xb.res, self.cst.res], writes=[tp.res])
                            if half == 0:
                                act.op(lambda e: e.copy(ot.t[:, 0:512], tp.t[:, :]), reads=[tp.res], writes=[ot.res])
                            else:
                                dve.op(lambda e: e.tensor_copy(ot.t[:, 512:1024], tp.t[:, :]), reads=[tp.res], writes=[ot.res], par=True)
                        r0 = t0 - (TC if last else 0) + sub * 128
                        sp.dma(self.out_loc[r0:r0 + 128, :], ot.t[:], reads=[ot.res], writes=[self.dres("out")], par=True)

def make_consts():
    c = np.zeros((128, 10, 128), np.float32)
    p = np.arange(128)[:, None]
    f = np.arange(128)[None, :]
    c[:, 0, :] = (p == f)
    c[:, 1, :] = 1.0
    c[:, 2, :] = (p // 64 == f // 64)
    rot = np.zeros((128, 128), np.float32)
    for i in range(128):
        if (i % 32) < 16:
            rot[i + 16, i] = -1.0
        else:
            rot[i - 16, i] = 1.0
    c[:, 3, :] = rot
    c[:, 4, :] = (p <= f)
    c[:, 5, :] = np.where(f < p, NEG, 0.0)
    c[:, 6, :] = np.where(f > p, NEG, 0.0)
    c[:, 7, :] = (p >= f)
    c[:, 8, :] = (f > p)
    c[:, 9, :] = (f < p)
    return c


def rope_tables(cfg, s):
    T, TC, S = cfg.T, cfg.TC, cfg.S
    i = np.arange(T)
    pos = i if s == 0 else (T - 1 - i)
    row = (pos // GRID_W).astype(np.float32)
    col = (pos % GRID_W).astype(np.float32)
    n_freq = 16
    inv = (np.float32(10000.0) ** (-np.arange(n_freq, dtype=np.float32) / n_freq)).astype(np.float32)
    ang_r = row[:, None] * inv
    ang_c = col[:, None] * inv
    ang = np.concatenate([ang_r, ang_r, ang_c, ang_c], axis=-1)
    cosT = np.ones((128, S), np.float32)
    sinT = np.zeros((128, S), np.float32)
    cosT[:, TC:] = np.concatenate([np.cos(ang), np.cos(ang)], axis=1).T
    sinT[:, TC:] = np.concatenate([np.sin(ang), np.sin(ang)], axis=1).T
    return cosT, sinT


def conv_taps(w, s):
    z = np.zeros_like(w[0])
    if s == 0:
        return np.stack([z, w[0], w[1], w[2], w[3]])
    return np.stack([w[3], w[2], w[1], w[0], z])


def prep_core(inp, core, cfg):
    L = cfg.depth
    b, s = core // 2, core % 2
    lat = inp["x"][b]
    cx = inp["ctx"][b]
    if s == 1:
        lat = lat[::-1]
        cx = cx[::-1]
    m = {}
    m["x_loc"] = np.ascontiguousarray(np.concatenate([cx, lat], axis=0), dtype=np.float32)
    cv = np.stack([inp["c"][b], inp["c_ctx"]], axis=-1)
    m["cvec"] = np.ascontiguousarray(cv.reshape(8, 128, 2).transpose(1, 0, 2))
    m["ada_w"] = np.ascontiguousarray(inp["ada_w"][:L])
    pvt = np.zeros((L, 128, NPV), np.float32)

    def chunks(v):
        return v.reshape(-1, 128).T

    for l in range(L):
        pvt[l, :, PV_N1G:PV_N1G + 8] = chunks(inp["norm1_g"][l])
        pvt[l, :, PV_N2G:PV_N2G + 8] = chunks(inp["norm2_g"][l])
        pvt[l, :, PV_ADAB:PV_ADAB + 48] = chunks(inp["ada_b"][l])
        pvt[l, :, PV_QG] = np.tile(inp["da_q_norm_g"][l], 2)
        pvt[l, :, PV_KG] = np.tile(inp["da_k_norm_g"][l], 2)
        pvt[l, :, PV_SUBG] = inp["da_sub_norm_g"][l]
        pvt[l, :64, PV_LAM:PV_LAM + 4] = inp["da_lambda"][l].T
        lt = conv_taps(inp["lru_conv_w"][l], s)
        for c in range(4):
            pvt[l, :, PV_LCW + c * 5:PV_LCW + c * 5 + 5] = lt[:, c * 128:(c + 1) * 128].T
        pvt[l, :, PV_LCB:PV_LCB + 4] = chunks(inp["lru_conv_b"][l])
        for d in range(2):
            od = d ^ s
            for g in range(2):
                pvt[l, :, PV_LGB + (d * 2 + g) * 4:PV_LGB + (d * 2 + g) * 4 + 4] = chunks(inp["lru_gate_b"][l, od, g])
            pvt[l, :, PV_LLAM + d * 4:PV_LLAM + d * 4 + 4] = chunks(inp["lru_lambda"][l, od])
            pvt[l, :, PV_ALOG + d * 4:PV_ALOG + d * 4 + 4] = inp["gdn_A_log"][l, od][None, :]
            pvt[l, :, PV_DTB + d * 4:PV_DTB + d * 4 + 4] = inp["gdn_dt_bias"][l, od][None, :]
        gtaps = conv_taps(inp["gdn_conv_w"][l], s)
        for c in range(12):
            pvt[l, :, PV_GCW + c * 5:PV_GCW + c * 5 + 5] = gtaps[:, c * 128:(c + 1) * 128].T
        pvt[l, :, PV_GNG] = inp["gdn_norm_g"][l]
    m["pvtab"] = pvt
    wi = inp["w_in"][:L]
    gb = wi[:, :, O_GB:O_GB + 8].reshape(L, D, 2, 4)
    ga = wi[:, :, O_GA:O_GA + 8].reshape(L, D, 2, 4)
    if s == 1:
        gb = gb[:, :, ::-1]
        ga = ga[:, :, ::-1]
    m["w_in"] = np.ascontiguousarray(np.concatenate([
        wi[:, :, O_K:O_K + 512], wi[:, :, O_LX:O_LX + 512], wi[:, :, O_GQKV:O_GQKV + 1536], wi[:, :, O_V:O_V + 512],
        gb.reshape(L, D, 8), ga.reshape(L, D, 8),
        wi[:, :, O_Q:O_Q + 512], wi[:, :, O_LY:O_LY + 512], wi[:, :, O_GZ:O_GZ + 512], wi[:, :, O_GATE:O_GATE + 3072]], axis=2))
    gw = np.zeros((L, 2, 2, 4, 128, 128), np.float32)
    for d in range(2):
        od = d ^ s
        for c in range(4):
            gw[:, d, :, c, 0:64, 0:64] = inp["lru_gate_w"][:L, od, :, 2 * c]
            gw[:, d, :, c, 64:128, 64:128] = inp["lru_gate_w"][:L, od, :, 2 * c + 1]
    m["lru_gw"] = gw
    m["w_branch"] = np.ascontiguousarray(inp["w_branch"][:L])
    m["w_out"] = np.ascontiguousarray(inp["w_out"][:L])
    m["w1"] = np.ascontiguousarray(inp["mlp_w1"][:L])
    m["w2"] = np.ascontiguousarray(inp["mlp_w2"][:L])
    m["ropec"], m["ropes"] = rope_tables(cfg, s)
    m["consts"] = make_consts()
    return m


_PROG_CACHE = {}


def _run_layer(inp, lidx, final, T, TC):
    B = inp["x"].shape[0]
    cfg = Cfg(T=T, TC=TC, depth=1)
    key = (T, TC, lidx, final)
    if key not in _PROG_CACHE:
        _PROG_CACHE[key] = Prog(cfg, lbase=lidx, final=final).build()
    nc = _PROG_CACHE[key]
    sub = dict(inp)
    for k in list(sub):
        if k not in ("x", "c", "ctx", "c_ctx"):
            sub[k] = sub[k][lidx:lidx + 1]
    ncores = 2 * B
    in_maps = [prep_core(sub, c, cfg) for c in range(ncores)]
    res = run_bass_kernel_spmd(nc, in_maps, core_ids=list(range(ncores)))
    return [np.asarray(res.results[c]["out_loc"]) for c in range(ncores)]


def kernel(**inputs):
    inp = {k: np.asarray(v) for k, v in inputs.items()}
    B, T, _ = inp["x"].shape
    TC = inp["ctx"].shape[1]
    depth = inp["ada_w"].shape[0]
    H = T // 2
    cur = dict(inp)
    for lidx in range(depth):
        final = (lidx == depth - 1)
        outs = _run_layer(cur, lidx, final, T, TC)
        x_new = np.empty((B, T, D), np.float32)
        ctx_new = np.empty((B, TC, D), np.float32)
        for c in range(2 * B):
            b, s = c // 2, c % 2
            o = outs[c]
            lat = o if final else o[TC:]
            if s == 0:
                x_new[b, :H] = lat
                if not final:
                    ctx_new[b] = o[:TC]
            else:
                x_new[b, H:] = lat[::-1]
        cur["x"] = x_new
        if not final:
            cur["ctx"] = ctx_new
    return cur["x"]
```
